# Optimizing a Trainium2 kernel written in Bass

```python
import jax, jax.numpy as jnp
from jax import lax
import numpy as np

D_MODEL = 1024
BATCH = 16
SEQ = 2048
DEPTH = 1

CONV_WIDTH = D_MODEL
CONV_KERNEL = 31
GLA_HEADS = 4
GLA_KEY_DIM = D_MODEL // 2
GLA_VAL_DIM = D_MODEL
GLA_DK = GLA_KEY_DIM // GLA_HEADS
GLA_DV = GLA_VAL_DIM // GLA_HEADS
GATE_RANK = 16
GATE_TEMP = 16.0
CHUNK = 64
EPS = 1e-6

IN_SIZES = (CONV_WIDTH, CONV_WIDTH, CONV_WIDTH,
            GLA_KEY_DIM, GLA_KEY_DIM, GLA_VAL_DIM, GATE_RANK, GLA_VAL_DIM,
            D_MODEL, D_MODEL)
IN_SPLITS = tuple(int(s) for s in np.cumsum(IN_SIZES)[:-1])
IN_WIDTH = int(sum(IN_SIZES))

kernel_name = "hybrid_conformer_conv_gla_gated_merge"


def rms_norm(x, g):
    xf = x.astype(jnp.float32)
    y = xf * lax.rsqrt(jnp.mean(xf * xf, axis=-1, keepdims=True) + EPS)
    return (y * g.astype(jnp.float32)).astype(x.dtype)


def layer_norm(x, g, b):
    xf = x.astype(jnp.float32)
    mu = jnp.mean(xf, axis=-1, keepdims=True)
    xc = xf - mu
    y = xc * lax.rsqrt(jnp.mean(xc * xc, axis=-1, keepdims=True) + EPS)
    return (y * g.astype(jnp.float32) + b.astype(jnp.float32)).astype(x.dtype)


def conformer_conv_branch(u_val, u_gate, z, conv_w, conv_b, ln_g, ln_b, w_proj):
    u = u_val * jax.nn.sigmoid(u_gate)
    u = lax.conv_general_dilated(
        u, conv_w[:, None, :].astype(u.dtype),
        window_strides=(1,), padding=[(CONV_KERNEL - 1, 0)],
        dimension_numbers=('NWC', 'WIO', 'NWC'),
        feature_group_count=CONV_WIDTH) + conv_b
    u = jax.nn.silu(layer_norm(u, ln_g, ln_b))
    u = u * jax.nn.silu(z)
    return u @ w_proj


def gla_chunked(q, k, v, log_a):
    B, S, H, DK = q.shape
    DV = v.shape[-1]
    N = S // CHUNK
    f32 = jnp.float32
    q = (q.astype(f32) * (DK ** -0.5)).reshape(B, N, CHUNK, H, DK)
    k = k.astype(f32).reshape(B, N, CHUNK, H, DK)
    v = v.astype(f32).reshape(B, N, CHUNK, H, DV)
    b = jnp.cumsum(log_a.astype(f32).reshape(B, N, CHUNK, H, DK), axis=2)
    b_last = b[:, :, -1]
    ref = b[:, :, CHUNK // 2 - 1:CHUNK // 2]
    scores = jnp.einsum('bnihd,bnjhd->bnhij', q * jnp.exp(b - ref), k * jnp.exp(ref - b))
    causal = jnp.tril(jnp.ones((CHUNK, CHUNK), dtype=bool))
    scores = jnp.where(causal, scores, 0.0)
    o_intra = jnp.einsum('bnhij,bnjhv->bnihv', scores, v)
    kv = jnp.einsum('bnjhd,bnjhv->bnhdv', k * jnp.exp(b_last[:, :, None] - b), v)
    decay = jnp.exp(b_last)

    def step(state, inp):
        dec, kv_n = inp
        return dec[..., None] * state + kv_n, state

    _, s_prev = lax.scan(step, jnp.zeros((B, H, DK, DV), f32),
                         (jnp.moveaxis(decay, 1, 0), jnp.moveaxis(kv, 1, 0)))
    s_prev = jnp.moveaxis(s_prev, 0, 1)
    o_inter = jnp.einsum('bnihd,bnhdv->bnihv', q * jnp.exp(b), s_prev)
    return (o_intra + o_inter).reshape(B, S, H, DV)


def setup_inputs(seed: int = 0) -> dict:
    key = jax.random.key(seed)
    ks = jax.random.split(key, 16)
    f32 = jnp.float32
    nrm = lambda k, shape, scale: jax.random.normal(k, shape, f32) * scale
    L = DEPTH
    return {
        "x": jax.random.normal(ks[0], (BATCH, SEQ, D_MODEL), f32),
        "norm_g": 1.0 + nrm(ks[1], (L, D_MODEL), 0.02),
        "w_in": nrm(ks[2], (L, D_MODEL, IN_WIDTH), D_MODEL ** -0.5),
        "conv_w": nrm(ks[3], (L, CONV_KERNEL, CONV_WIDTH), CONV_KERNEL ** -0.5),
        "conv_b": nrm(ks[4], (L, CONV_WIDTH), 0.02),
        "conv_ln_g": 1.0 + nrm(ks[5], (L, CONV_WIDTH), 0.02),
        "conv_ln_b": nrm(ks[6], (L, CONV_WIDTH), 0.02),
        "w_conv_out": nrm(ks[7], (L, CONV_WIDTH, D_MODEL), CONV_WIDTH ** -0.5),
        "gate_w2": nrm(ks[8], (L, GATE_RANK, GLA_KEY_DIM), GATE_RANK ** -0.5),
        "gate_b": nrm(ks[9], (L, GLA_KEY_DIM), 0.1),
        "gla_norm_g": 1.0 + nrm(ks[10], (L, GLA_DV), 0.02),
        "w_gla_out": nrm(ks[11], (L, GLA_VAL_DIM, D_MODEL), GLA_VAL_DIM ** -0.5),
        "w_out": nrm(ks[12], (L, D_MODEL, D_MODEL), D_MODEL ** -0.5),
        "final_g": 1.0 + nrm(ks[13], (D_MODEL,), 0.02),
    }


def reference(x, norm_g, w_in, conv_w, conv_b, conv_ln_g, conv_ln_b, w_conv_out,
              gate_w2, gate_b, gla_norm_g, w_gla_out, w_out, final_g):
    B, S, _ = x.shape
    for l in range(DEPTH):
        h = rms_norm(x, norm_g[l])
        proj = h @ w_in[l]
        (c_val, c_gate, c_z, q, k, v, g_lr, g_r, m_conv, m_gla) = jnp.split(proj, IN_SPLITS, axis=-1)

        y_conv = conformer_conv_branch(c_val, c_gate, c_z, conv_w[l], conv_b[l],
                                       conv_ln_g[l], conv_ln_b[l], w_conv_out[l])

        z = (g_lr @ gate_w2[l] + gate_b[l]).astype(jnp.float32)
        log_a = jax.nn.log_sigmoid(z) / GATE_TEMP
        o = gla_chunked(q.reshape(B, S, GLA_HEADS, GLA_DK),
                        k.reshape(B, S, GLA_HEADS, GLA_DK),
                        v.reshape(B, S, GLA_HEADS, GLA_DV),
                        log_a.reshape(B, S, GLA_HEADS, GLA_DK))
        o = rms_norm(o, gla_norm_g[l]).astype(x.dtype).reshape(B, S, GLA_VAL_DIM)
        y_gla = (o * jax.nn.silu(g_r)) @ w_gla_out[l]

        y = jax.nn.sigmoid(m_conv) * y_conv + jax.nn.sigmoid(m_gla) * y_gla
        x = x + y @ w_out[l]
    return rms_norm(x, final_g)
```

```python
import numpy as np
from contextlib import ExitStack
import concourse.bass as bass
import concourse.mybir as mybir
from concourse.bass_utils import run_bass_kernel_spmd

F32 = mybir.dt.float32
BF16 = mybir.dt.bfloat16
AF = mybir.ActivationFunctionType
ALU = mybir.AluOpType

ENGS = ("pe", "act", "dve", "pool", "sp")

D = 1024
SEQ = 2048
NCORE = 8
TOK_CORE = 4096
T = 512
NT = TOK_CORE // T
TPS = SEQ // T
EPS = 1e-6
NR = 10
KTAPS = 31

U_GLR = 0
U_CG = lambda c: 1 + 2 * c
U_CV = lambda c: 2 + 2 * c
U_CZ = lambda c: 17 + c
U_GR = lambda c: 25 + c
U_Q = lambda h: 33 + h
U_K = lambda h: 37 + h
U_MC = lambda c: 41 + c
U_WCO = lambda c: 49 + 3 * c
U_MG = lambda c: 50 + 3 * c
U_WGO = lambda c: 51 + 3 * c
N_NAR = 73
N_WIDE = 4


class Buf:
    __slots__ = ("name", "w", "r")

    def __init__(self, name):
        self.name = name
        self.w = None
        self.r = {}


class Op:
    __slots__ = ("eng", "fn", "waits", "marked", "clock", "idx", "dma_sem", "dma_val")


class Sched:
    def __init__(self):
        self.prog = {e: [] for e in ENGS}
        self.clock = {e: {} for e in ENGS}
        self.dma_cnt = {}
        self._dma_clock = {}
        self.n_waits = 0

    def op(self, eng, fn, reads=(), writes=(), dma_sem=None):
        is_dma = dma_sem is not None
        deps = {}

        def add(tok, raw):
            if tok is None:
                return
            if tok[0] == "e" and tok[1] == eng and not is_dma and not raw and eng == "pe":
                return
            k = (tok[0], tok[1])
            if deps.get(k, -1) < tok[2]:
                deps[k] = tok[2]

        for b in reads:
            add(b.w, True)
        for b in writes:
            add(b.w, False)
            for kk, vv in b.r.items():
                add((kk[0], kk[1], vv), False)
        clk = self.clock[eng]
        o = Op()
        o.eng = eng
        o.fn = fn
        o.waits = []
        o.marked = False
        o.dma_sem = dma_sem
        o.idx = len(self.prog[eng])
        for k, v in deps.items():
            if clk.get(k, -1) >= v:
                continue
            o.waits.append((k, v))
            self.n_waits += 1
            if k[0] == "e":
                src = self.prog[k[1]][v]
                src.marked = True
                src_clock = src.clock
            else:
                src_clock = self._dma_clock[(k[1], v)]
            for kk, vv in src_clock.items():
                if clk.get(kk, -1) < vv:
                    clk[kk] = vv
            if clk.get(k, -1) < v:
                clk[k] = v
        if is_dma:
            val = self.dma_cnt.get(dma_sem, 0) + 16
            self.dma_cnt[dma_sem] = val
            o.dma_val = val
            tok = ("d", dma_sem, val)
            self._dma_clock[(dma_sem, val)] = dict(clk)
        else:
            tok = ("e", eng, o.idx)
        o.clock = dict(clk)
        self.prog[eng].append(o)
        for b in reads:
            b.r[(tok[0], tok[1])] = tok[2]
        for b in writes:
            b.w = tok
            b.r = {}
        return tok

    def finalize_counts(self):
        self.cnts = {}
        for e in ENGS:
            c = 0
            lst = []
            for o in self.prog[e]:
                if o.marked:
                    c += 1
                lst.append(c)
            self.cnts[e] = lst

    def replay(self, eng, engobj, eng_sems, dma_sems):
        for o in self.prog[eng]:
            for k, v in o.waits:
                if k[0] == "e":
                    engobj.wait_ge(eng_sems[k[1]], self.cnts[k[1]][v])
                else:
                    engobj.wait_ge(dma_sems[k[1]], v)
            if o.fn is None:
                continue
            inst = o.fn(engobj)
            if o.dma_sem is not None:
                inst.then_inc(dma_sems[o.dma_sem], 16)
            elif o.marked:
                inst.then_inc(eng_sems[eng], 1)


def build_nc(n_tiles=NT):
    nc = bass.Bass("TRN2", target_bir_lowering=False)
    x_d = nc.dram_tensor("x", [TOK_CORE, D], F32, kind="ExternalInput").ap()
    wnar_f = nc.dram_tensor("wnar", [N_NAR, 128, 1024], F32, kind="ExternalInput").ap()
    wwide_f = nc.dram_tensor("wwide", [N_WIDE, 128, 4096], F32, kind="ExternalInput").ap()
    pp_d = nc.dram_tensor("pp", [128, 32], F32, kind="ExternalInput").ap()
    cw_d = nc.dram_tensor("cw", [128, 8 * 4 * 8], F32, kind="ExternalInput").ap()
    gw2b_d = nc.dram_tensor("gw2b", [32, 512], F32, kind="ExternalInput").ap()
    ng_d = nc.dram_tensor("ng", [D], F32, kind="ExternalInput").ap()
    fg_d = nc.dram_tensor("fg", [D], F32, kind="ExternalInput").ap()
    cst_d = nc.dram_tensor("cst", [128, 4 * 128 + 512 + 256], F32, kind="ExternalInput").ap()
    out_d = nc.dram_tensor("out", [TOK_CORE, D], F32, kind="ExternalOutput").ap()
    wnar_b = nc.dram_tensor("wnar_b", [N_NAR, 128, 1024], BF16, kind="Internal").ap()
    wwide_b = nc.dram_tensor("wwide_b", [N_WIDE, 128, 4096], BF16, kind="Internal").ap()
    usc_t = nc.dram_tensor("usc", [4, 128, 544], BF16, kind="Internal")
    usc = usc_t.ap()

    S = Sched()
    dma_sem_names = []

    def newsem(name):
        dma_sem_names.append(name)
        return name

    with ExitStack() as es:
        def sb(name, shape, dt):
            return es.enter_context(nc.sbuf_tensor("sb_" + name, shape, dt))

        def ps(name, shape, dt):
            return es.enter_context(nc.psum_tensor("ps_" + name, shape, dt))

        cst = sb("cst", [128, 4 * 128 + 512 + 256], F32)
        identf = cst[:, 0:128]
        Lf = cst[:, 128:256]
        Lr = cst[:, 256:384]
        Ur = cst[:, 384:512]
        maskT = cst[:, 512:1024]
        identb = sb("identb", [128, 128], BF16)
        identstack = sb("identstack", [128, 8, 32], BF16)
        onesb = sb("onesb", [128, 128], BF16)
        Gn = sb("Gn", [128, D], F32)
        Gf = sb("Gf", [128, D], F32)
        pp = sb("pp", [128, 32], F32)
        cw = sb("cw", [128, 8, 4, 8], F32)
        gw2b = sb("gw2b", [32, 512], BF16)
        glr = sb("glr", [32, 512], BF16)
        wn = sb("wn", [128, NR, 1024], BF16)
        ww = sb("ww", [128, 2, 4096], BF16)
        W4 = sb("W4", [128, 4, 4, 8, 32], BF16)
        xa = sb("xa", [128, 2, D], F32)
        hb = sb("hb", [128, 4, D], BF16)
        hT = sb("hT", [128, 2, 8, T], BF16)
        u = sb("u", [128, 4, 544], BF16)
        u4 = sb("u4", [128, 4, 4, 544], BF16)
        halo = sb("halo", [128, 8, 32], BF16)
        U1 = sb("U1", [128, 8, T], F32)
        aconv = sb("aconv", [128, 8, T], BF16)
        sgr = sb("sgr", [128, 8, T], BF16)
        vT = sb("vT", [128, 4, D], BF16)
        agla = sb("agla", [128, 8, T], BF16)
        yb = sb("yb", [128, 8, T], BF16)
        Sst = sb("Sst", [128, 4, 256], F32)
        Sbf = sb("Sbf", [128, 2, 4, 256], BF16)
        ssx = sb("ssx", [128, 16], F32)
        FP = sb("FP", [128, 10, 512], F32)
        HP = sb("HP", [128, 8, 512], BF16)
        pb = [ps("pb%d" % i, [128, 512], F32) for i in range(6)]
        pb6 = ps("pb6", [128, 1024], BF16)
        pb7 = ps("pb7", [128, 512], F32)
        pb = pb + [pb6, pb7]

        eng_sems = {e: es.enter_context(nc.semaphore("s_" + e)) for e in ENGS}

        B = {}

        def mk(name, n=None):
            if n is None:
                B[name] = Buf(name)
            else:
                B[name] = [Buf("%s%d" % (name, i)) for i in range(n)]

        for nm in ["cst", "identb", "identstack", "onesb", "Gn", "Gf", "pp", "cw", "gw2b", "glr",
                   "halo", "Sst"]:
            mk(nm)
        mk("ssx", 16)
        mk("wn", NR); mk("ww", 2); mk("W4", 4); mk("xa", 2); mk("hb", 4); mk("hT", 2); mk("u", 4)
        mk("U1", 8); mk("aconv", 8); mk("sgr", 8); mk("vT", 4); mk("agla", 8); mk("yb", 8)
        mk("Sbf", 2); mk("FP", 10); mk("HP", 8); mk("pb", 8)
        mk("wnar_b", N_NAR); mk("wwide_b", N_WIDE)
        mk("outdone", 2)
        B["u4"] = [[Buf("u4_%d_%d" % (i, g)) for g in range(4)] for i in range(4)]
        mk("usc", 4)

        PB = B["pb"]

        def pbank(i):
            if i < 6:
                return pb[i]
            return pb6 if i == 6 else pb7

        S.op("sp", lambda e: e.dma_start(out=cst[:], in_=cst_d), writes=[B["cst"]], dma_sem=newsem("c_cst"))
        S.op("sp", lambda e: e.dma_start(out=pp[:], in_=pp_d), writes=[B["pp"]], dma_sem=newsem("c_pp"))
        S.op("sp", lambda e: e.dma_start(out=cw[:].rearrange("p a b c -> p (a b c)"), in_=cw_d), writes=[B["cw"]],
             dma_sem=newsem("c_cw"))
        S.op("sp", lambda e: e.dma_start(out=Gn[:], in_=ng_d.partition_broadcast(128)), writes=[B["Gn"]],
             dma_sem=newsem("c_gn"))
        S.op("sp", lambda e: e.dma_start(out=Gf[:], in_=fg_d.partition_broadcast(128)), writes=[B["Gf"]],
             dma_sem=newsem("c_gf"))
        S.op("pool", lambda e: e.dma_start(out=gw2b[:], in_=gw2b_d), writes=[B["gw2b"]], dma_sem=newsem("c_gw"))

        nar_groups = [(0, 1), (1, 3), (3, 7), (7, 11), (11, 17), (17, 21), (21, 25), (25, 29), (29, 33)]
        wide_first = [(0, 1), (1, 2)]
        nar_groups2 = [(33, 37), (37, 41), (41, 45), (45, 49), (49, 53), (53, 57), (57, 61), (61, 65),
                       (65, 69), (69, 73)]
        wide_second = [(2, 3), (3, 4)]

        def cast_nar(a, b_):
            S.op("pool", lambda e: e.dma_start(out=wnar_b[a:b_], in_=wnar_f[a:b_]),
                 writes=B["wnar_b"][a:b_], dma_sem=newsem("cn%d" % a))

        def cast_wide(a, b_):
            S.op("pool", lambda e: e.dma_start(out=wwide_b[a:b_], in_=wwide_f[a:b_]),
                 writes=B["wwide_b"][a:b_], dma_sem=newsem("cw%d" % a))

        def issue_casts():
            for (a, b_) in nar_groups:
                cast_nar(a, b_)
            for (a, b_) in wide_first:
                cast_wide(a, b_)
            for (a, b_) in nar_groups2:
                cast_nar(a, b_)
            for (a, b_) in wide_second:
                cast_wide(a, b_)

        S.op("dve", lambda e: e.tensor_copy(out=identb[:], in_=identf), reads=[B["cst"]], writes=[B["identb"]])
        S.op("dve", lambda e: e.tensor_copy(out=identstack[:].rearrange("p a b -> p (a b)"), in_=cst[:, 1024:1280]),
             reads=[B["cst"]], writes=[B["identstack"]])
        S.op("pool", lambda e: e.memset(u[:], 0.0), writes=B["u"])
        S.op("pool", lambda e: e.memset(onesb[:], 1.0), writes=[B["onesb"]])
        S.op("pool", lambda e: e.memset(glr[:], 1.0), writes=[B["glr"]])

        nar_sems = [newsem("wn%d" % i) for i in range(NR)]
        wide_sems = [newsem("ww%d" % i) for i in range(2)]
        nar_loaded = [0]
        wide_loaded = [0]
        n_nar_total = n_tiles * N_NAR
        n_wide_total = n_tiles * N_WIDE

        def nar_ensure(g):
            lim = min(g + NR, n_nar_total)
            while nar_loaded[0] < lim:
                m = nar_loaded[0]
                slot = m % NR
                uid = m % N_NAR
                S.op("sp", lambda e, slot=slot, uid=uid: e.dma_start(out=wn[:, slot, :], in_=wnar_b[uid]),
                     reads=[B["wnar_b"][uid]], writes=[B["wn"][slot]], dma_sem=nar_sems[slot])
                nar_loaded[0] += 1

        def wide_ensure(lim):
            lim = min(lim, n_wide_total)
            while wide_loaded[0] < lim:
                m = wide_loaded[0]
                slot = m % 2
                uid = m % N_WIDE
                S.op("sp", lambda e, slot=slot, uid=uid: e.dma_start(out=ww[:, slot, :], in_=wwide_b[uid]),
                     reads=[B["wwide_b"][uid]], writes=[B["ww"][slot]], dma_sem=wide_sems[slot])
                wide_loaded[0] += 1

        nar_last = [-1]

        def nar(t, uid):
            g = t * N_NAR + uid
            assert g == nar_last[0] + 1, (g, nar_last[0])
            nar_last[0] = g
            nar_ensure(g)
            slot = g % NR
            return wn[:, slot, :], B["wn"][slot]

        def wide(t, uid):
            g = t * N_WIDE + uid
            wide_ensure(g + 1)
            slot = g % 2
            return ww[:, slot, :], B["ww"][slot]

        pj = {"set": [0, 1], "i": 0}

        def set_pbanks(banks):
            pj["set"] = list(banks)
            pj["i"] = 0

        def next_pbank():
            pj["i"] = (pj["i"] + 1) % len(pj["set"])
            return pj["set"][pj["i"]]

        def proj_fm(t, uid, hbuf, coltile=False):
            wap, wbuf = nar(t, uid)
            bi = next_pbank()
            for kc in range(8):
                if not coltile:
                    S.op("pe", lambda e, bi=bi, kc=kc, wap=wap: e.matmul(
                        pb[bi][:], lhsT=wap[:, kc * 128:(kc + 1) * 128], rhs=hT[:, hbuf, kc, :],
                        start=(kc == 0), stop=(kc == 7)),
                        reads=[wbuf, B["hT"][hbuf]], writes=[PB[bi]])
                else:
                    for b in range(4):
                        S.op("pe", lambda e, bi=bi, kc=kc, wap=wap, b=b: e.matmul(
                            pb[bi][32 * b:32 * b + 32, :],
                            lhsT=wap[:, kc * 128 + 32 * b:kc * 128 + 32 * b + 32], rhs=hT[:, hbuf, kc, :],
                            start=(kc == 0), stop=(kc == 7), tile_position=(0, 32 * b)),
                            reads=[wbuf, B["hT"][hbuf]], writes=[PB[bi]])
            return bi

        def proj_from(t, uid, src, srcbufs):
            wap, wbuf = nar(t, uid)
            bi = next_pbank()
            for kc in range(8):
                S.op("pe", lambda e, bi=bi, kc=kc, wap=wap: e.matmul(
                    pb[bi][:], lhsT=wap[:, kc * 128:(kc + 1) * 128], rhs=src[:, kc, :],
                    start=(kc == 0), stop=(kc == 7)),
                    reads=[wbuf] + list(srcbufs), writes=[PB[bi]])
            return bi

        xa_sems = [newsem("xa0"), newsem("xa1")]
        pre_cnt = [0]

        def pre_elem(t, s):
            g = t * 4 + s
            p = g % 2
            hs = s
            r0 = t * T + s * 128
            S.op("act", lambda e, p=p, r0=r0: e.dma_start(out=xa[:, p, :], in_=x_d[r0:r0 + 128, :]),
                 writes=[B["xa"][p]], dma_sem=xa_sems[p])
            col = g % 8
            S.op("act", lambda e, p=p, hs=hs, col=col: e.activation(
                out=hb[:, hs, :], in_=xa[:, p, :], func=AF.Square, accum_out=ssx[:, col:col + 1]),
                reads=[B["xa"][p]], writes=[B["hb"][hs], B["ssx"][col]])
            S.op("pool", lambda e, col=col: e.tensor_scalar(
                out=ssx[:, col:col + 1], in0=ssx[:, col:col + 1], scalar1=1.0 / D, scalar2=EPS,
                op0=ALU.mult, op1=ALU.add), reads=[B["ssx"][col]], writes=[B["ssx"][col]])
            S.op("pool", lambda e, col=col: e.tensor_tensor(
                out=ssx[:, col:col + 1], in0=ssx[:, col:col + 1], in1=pp[:, 26:27], op=ALU.pow),
                reads=[B["ssx"][col], B["pp"]], writes=[B["ssx"][col]])
            S.op("dve", lambda e, p=p, hs=hs, col=col: e.scalar_tensor_tensor(
                out=hb[:, hs, :], in0=xa[:, p, :], scalar=ssx[:, col:col + 1], in1=Gn[:],
                op0=ALU.mult, op1=ALU.mult),
                reads=[B["xa"][p], B["ssx"][col], B["Gn"]], writes=[B["hb"][hs]])

        def pre_pe(t, s):
            p = s
            hbuf = t % 2
            for c in range(8):
                S.op("pe", lambda e, p=p, c=c: e.transpose(
                    out=pb6[:, c * 128:(c + 1) * 128], in_=hb[:, p, c * 128:(c + 1) * 128], identity=identb[:]),
                    reads=[B["hb"][p], B["identb"]], writes=[PB[6]])
            S.op("dve", lambda e, hbuf=hbuf, s=s: e.tensor_copy(
                out=hT[:, hbuf, :, s * 128:(s + 1) * 128],
                in_=pb6[:].rearrange("p (c t) -> p c t", c=8)),
                reads=[PB[6]], writes=[B["hT"][hbuf]])

        def preamble_a(t):
            for s in range(4):
                pre_elem(t, s)
                pre_pe(t, s)

        xr_sems = [newsem("xr0"), newsem("xr1")]
        out_sems = [newsem("o0"), newsem("o1")]
        cnt = {"ub": 0, "db": 0, "cb": 0, "sg": 0, "fin": 0}
        u4_sems = [[newsem("u4s_%d_%d" % (i, g)) for g in range(4)] for i in range(4)]
        usc_sems = [newsem("usc%d" % i) for i in range(4)]

        def tile(t):
            hbuf = t % 2
            seq_start = (t % TPS == 0)
            if seq_start:
                S.op("pool", lambda e: e.memset(halo[:], 0.0), writes=[B["halo"]])
                S.op("pool", lambda e: e.memset(Sst[:], 0.0), writes=[B["Sst"]])
                sb0 = (t * 4) % 2
                S.op("pool", lambda e, sb0=sb0: e.memset(Sbf[:, sb0, :, :], 0.0), writes=[B["Sbf"][sb0]])

            bi = proj_fm(t, U_GLR, hbuf)
            S.op("act", lambda e, bi=bi: e.activation(out=glr[0:16, :], in_=pb[bi][0:16, :], func=AF.Copy),
                 reads=[PB[bi]], writes=[B["glr"]])

            set_pbanks([0, 1, 7])
            s2 = {}

            def s2_A(c):
                ub = cnt["ub"] % 4
                cnt["ub"] += 1
                db = cnt["db"] % 4
                cnt["db"] += 1
                s2[c] = (ub, db)
                for b in range(4):
                    S.op("pool", lambda e, db=db, c=c, b=b: e.tensor_tensor(
                        out=W4[:, db, b, :, :], in0=identstack[:],
                        in1=cw[:, c, b, :].unsqueeze(2).to_broadcast([128, 8, 32]), op=ALU.mult),
                        reads=[B["identstack"], B["cw"]], writes=[B["W4"][db]])
                bg = proj_fm(t, U_CG(c), hbuf, coltile=True)
                sgi = c % 2
                S.op("act", lambda e, bg=bg, sgi=sgi: e.activation(out=FP[:, sgi, :], in_=pb[bg][:], func=AF.Sigmoid),
                     reads=[PB[bg]], writes=[B["FP"][sgi]])
                bv = proj_fm(t, U_CV(c), hbuf, coltile=True)
                S.op("pool", lambda e, ub=ub, c=c: e.tensor_copy(out=u[:, ub, 0:30], in_=halo[:, c, 0:30]),
                     reads=[B["halo"]], writes=[B["u"][ub]])
                S.op("dve", lambda e, bv=bv, sgi=sgi, ub=ub: e.tensor_tensor(
                    out=u[:, ub, 30:542], in0=pb[bv][:], in1=FP[:, sgi, :], op=ALU.mult),
                    reads=[PB[bv], B["FP"][sgi]], writes=[B["u"][ub]])
                S.op("pool", lambda e, ub=ub, c=c: e.tensor_copy(out=halo[:, c, 0:30], in_=u[:, ub, 512:542]),
                     reads=[B["u"][ub]], writes=[B["halo"]])
                S.op("sp", lambda e, ub=ub: e.dma_start(out=usc[ub], in_=u[:, ub, :]),
                     reads=[B["u"][ub]], writes=[B["usc"][ub]], dma_sem=usc_sems[ub])

            def s2_R(c):
                ub, db = s2[c]
                for g in range(4):
                    src = bass.AP(usc_t, ub * 128 * 544 + g, [[544, 32], [32 * 544, 4], [1, 540]])
                    S.op("sp", lambda e, ub=ub, g=g, src=src: e.dma_start(
                        out=u4[32 * g:32 * g + 32, ub, :, 0:540], in_=src),
                        reads=[B["usc"][ub]], writes=[B["u4"][ub][g]], dma_sem=u4_sems[ub][g])

            def s2_B(c):
                ub, db = s2[c]
                cbk = 2 + (c % 2)
                for j in range(8):
                    for b in range(4):
                        S.op("pe", lambda e, cbk=cbk, db=db, j=j, b=b, ub=ub: e.matmul(
                            pb[cbk][32 * b:32 * b + 32, :], lhsT=W4[:, db, b, j, :],
                            rhs=u4[:, ub, b, 4 * j:4 * j + 512],
                            start=(j == 0), stop=(j == 7), tile_position=(0, 32 * b)),
                            reads=[B["W4"][db]] + B["u4"][ub], writes=[PB[cbk]])
                S.op("act", lambda e, cbk=cbk, c=c: e.activation(
                    out=U1[:, c, :], in_=pb[cbk][:], func=AF.Identity, bias=pp[:, c:c + 1]),
                    reads=[PB[cbk], B["pp"]], writes=[B["U1"][c]])
                vsq = 2 + c % 2
                S.op("act", lambda e, cbk=cbk, c=c, vsq=vsq: e.activation(
                    out=HP[:, vsq, :], in_=pb[cbk][:], func=AF.Square, bias=pp[:, c:c + 1]),
                    reads=[PB[cbk], B["pp"]], writes=[B["HP"][vsq]])
                vbi = c % 2
                S.op("act", lambda e, cbk=cbk, c=c, vbi=vbi: e.activation(
                    out=HP[:, vbi, :], in_=pb[cbk][:], func=AF.Identity, bias=pp[:, c:c + 1]),
                    reads=[PB[cbk], B["pp"]], writes=[B["HP"][vbi]])

            def s2_C(c):
                vsq = 2 + c % 2
                vbi = c % 2
                for b in range(4):
                    S.op("pe", lambda e, vbi=vbi, c=c, b=b: e.matmul(
                        pb[4][32 * b:32 * b + 32, :], lhsT=onesb[:, 32 * b:32 * b + 32], rhs=HP[:, vbi, :],
                        start=(c == 0), stop=(c == 7), tile_position=(0, 32 * b)),
                        reads=[B["onesb"], B["HP"][vbi]], writes=[PB[4]])
                for b in range(4):
                    S.op("pe", lambda e, vsq=vsq, c=c, b=b: e.matmul(
                        pb[5][32 * b:32 * b + 32, :], lhsT=onesb[:, 32 * b:32 * b + 32], rhs=HP[:, vsq, :],
                        start=(c == 0), stop=(c == 7), tile_position=(0, 32 * b)),
                        reads=[B["onesb"], B["HP"][vsq]], writes=[PB[5]])

            LAG = 3
            S.op("pe", lambda e: e.drain())
            for i in range(8 + LAG + 1):
                if t + 1 < n_tiles and i == 4:
                    pre_elem(t + 1, 0)
                if t + 1 < n_tiles and i == 6:
                    pre_elem(t + 1, 1)
                if i < 8:
                    s2_A(i)
                if 0 <= i - 1 < 8:
                    s2_R(i - 1)
                if 0 <= i - LAG < 8:
                    s2_B(i - LAG)
                if 0 <= i - LAG - 1 < 8:
                    s2_C(i - LAG - 1)

            S.op("pe", lambda e: e.drain())
            MU, RS = 2, 3
            S.op("dve", lambda e: e.tensor_scalar(out=FP[:, MU, :], in0=pb[4][:], scalar1=1.0 / D, scalar2=0.0,
                                                  op0=ALU.mult, op1=ALU.add),
                 reads=[PB[4]], writes=[B["FP"][MU]])
            S.op("dve", lambda e: e.tensor_tensor(out=FP[:, RS, :], in0=FP[:, MU, :], in1=FP[:, MU, :], op=ALU.mult),
                 reads=[B["FP"][MU]], writes=[B["FP"][RS]])
            S.op("dve", lambda e: e.scalar_tensor_tensor(out=FP[:, RS, :], in0=pb[5][:], scalar=1.0 / D,
                                                         in1=FP[:, RS, :], op0=ALU.mult, op1=ALU.subtract),
                 reads=[PB[5], B["FP"][RS]], writes=[B["FP"][RS]])

            set_pbanks([0, 1, 2, 3])
            wide_ensure(t * N_WIDE + 2)
            for c in range(8):
                if t + 1 < n_tiles:
                    if c == 0:
                        pre_elem(t + 1, 2)
                    if c == 2:
                        pre_elem(t + 1, 3)
                    if c >= 4:
                        pre_pe(t + 1, c - 4)
                bz = proj_fm(t, U_CZ(c), hbuf)
                S.op("act", lambda e, bz=bz, c=c: e.activation(out=aconv[:, c, :], in_=pb[bz][:], func=AF.Silu),
                     reads=[PB[bz]], writes=[B["aconv"][c]])

            for c in range(8):
                bg = proj_fm(t, U_GR(c), hbuf)
                S.op("act", lambda e, bg=bg, c=c: e.activation(out=sgr[:, c, :], in_=pb[bg][:], func=AF.Silu),
                     reads=[PB[bg]], writes=[B["sgr"][c]])

            S.op("act", lambda e: e.activation(out=FP[:, RS, :], in_=FP[:, RS, :], func=AF.Ln, bias=pp[:, 27:28]),
                 reads=[B["FP"][RS], B["pp"]], writes=[B["FP"][RS]])
            S.op("act", lambda e: e.activation(out=FP[:, RS, :], in_=FP[:, RS, :], func=AF.Exp, scale=-0.5),
                 reads=[B["FP"][RS]], writes=[B["FP"][RS]])

            def s4b(c):
                ti = 4 + c % 4
                S.op("pool", lambda e, c=c, ti=ti: e.tensor_tensor(out=FP[:, ti, :], in0=U1[:, c, :], in1=FP[:, MU, :],
                                                                   op=ALU.subtract),
                     reads=[B["U1"][c], B["FP"][MU]], writes=[B["FP"][ti]])
                S.op("dve", lambda e, ti=ti: e.tensor_tensor(out=FP[:, ti, :], in0=FP[:, ti, :], in1=FP[:, RS, :],
                                                             op=ALU.mult),
                     reads=[B["FP"][ti], B["FP"][RS]], writes=[B["FP"][ti]])
                S.op("act", lambda e, c=c, ti=ti: e.activation(out=FP[:, ti, :], in_=FP[:, ti, :], func=AF.Silu,
                                                               scale=pp[:, 8 + c:9 + c], bias=pp[:, 16 + c:17 + c]),
                     reads=[B["FP"][ti], B["pp"]], writes=[B["FP"][ti]])
                S.op("dve", lambda e, c=c, ti=ti: e.tensor_tensor(out=aconv[:, c, :], in0=FP[:, ti, :],
                                                                  in1=aconv[:, c, :], op=ALU.mult),
                     reads=[B["FP"][ti], B["aconv"][c]], writes=[B["aconv"][c]])

            def qk_proj(i):
                if i < 4:
                    h = i
                    bq = proj_fm(t, U_Q(h), hbuf)
                    S.op("act", lambda e, bq=bq, h=h: e.activation(out=U1[:, h, :], in_=pb[bq][:], func=AF.Copy,
                                                                   scale=float(128 ** -0.5)),
                         reads=[PB[bq]], writes=[B["U1"][h]])
                else:
                    h = i - 4
                    bk = proj_fm(t, U_K(h), hbuf)
                    S.op("act", lambda e, bk=bk, h=h: e.activation(out=U1[:, 4 + h, :], in_=pb[bk][:], func=AF.Copy),
                         reads=[PB[bk]], writes=[B["U1"][4 + h]])

            for i in range(8):
                s4b(i)
                if i >= 1:
                    qk_proj(i - 1)
            qk_proj(7)

            LP, EQ, EK, EB, EKD, OS0, OS1, RO, TN0, TN1 = 0, 1, 4, 5, 6, 7, 8, 9, 2, 3
            QS, QB, KS, KD, ATM, OQ0, OQ1 = 0, 1, 2, 3, 4, 5, 6
            set_pbanks([0, 1, 7])
            mc_next = [0]

            def vproj(s_):
                for half in range(2):
                    wap, wbuf = wide(t, half)
                    bi = next_pbank()
                    for kc in range(8):
                        S.op("pe", lambda e, bi=bi, kc=kc, s=s_, wap=wap: e.matmul(
                            pb[bi][:], lhsT=hT[:, hbuf, kc, s * 128:(s + 1) * 128],
                            rhs=wap[:, kc * 512:(kc + 1) * 512], start=(kc == 0), stop=(kc == 7)),
                            reads=[wbuf, B["hT"][hbuf]], writes=[PB[bi]])
                    if half == 0:
                        S.op("dve", lambda e, bi=bi, s=s_, half=half: e.tensor_copy(
                            out=vT[:, s, half * 512:(half + 1) * 512], in_=pb[bi][:]),
                            reads=[PB[bi]], writes=[B["vT"][s_]])
                    else:
                        S.op("act", lambda e, bi=bi, s=s_, half=half: e.activation(
                            out=vT[:, s, half * 512:(half + 1) * 512], in_=pb[bi][:], func=AF.Copy),
                            reads=[PB[bi]], writes=[B["vT"][s_]])

            def mc_raw():
                c = mc_next[0]
                if c >= 8:
                    return
                mc_next[0] += 1
                bm = proj_fm(t, U_MC(c), hbuf)
                S.op("act", lambda e, bm=bm, c=c: e.activation(out=yb[:, c, :], in_=pb[bm][:], func=AF.Copy),
                     reads=[PB[bm]], writes=[B["yb"][c]])

            def fill(point, s_):
                if point == "A":
                    if s_ + 1 < 4:
                        vproj(s_ + 1)
                    else:
                        mc_raw()
                    if s_ == 2:
                        wide_ensure(t * N_WIDE + 4)
                elif point == "B":
                    mc_raw()
                elif point == "C":
                    mc_raw()
                elif point == "D":
                    if s_ >= 2:
                        mc_raw()

            vproj(0)
            for s in range(4):
                g = t * 4 + s
                sbr = g % 2
                sbw = (g + 1) % 2
                cs = slice(s * 128, (s + 1) * 128)
                S.op("pe", lambda e, cs=cs: e.matmul(pb[2][:], lhsT=glr[0:32, cs], rhs=gw2b[0:32, :],
                                                     start=True, stop=True),
                     reads=[B["glr"], B["gw2b"]], writes=[PB[2]])
                S.op("act", lambda e: e.activation(out=FP[:, LP, :], in_=pb[2][:], func=AF.Exp, scale=-1.0),
                     reads=[PB[2]], writes=[B["FP"][LP]])
                S.op("act", lambda e: e.activation(out=FP[:, LP, :], in_=FP[:, LP, :], func=AF.Ln, bias=1.0),
                     reads=[B["FP"][LP]], writes=[B["FP"][LP]])
                fill("A", s)
                for h in range(4):
                    S.op("pe", lambda e, h=h: e.matmul(pb[3][:, h * 128:(h + 1) * 128],
                                                       lhsT=FP[:, LP, h * 128:(h + 1) * 128], rhs=Lr,
                                                       start=True, stop=True),
                         reads=[B["FP"][LP], B["cst"]], writes=[PB[3]])
                for h in range(4):
                    S.op("pe", lambda e, h=h: e.matmul(pb[4][:, h * 128:(h + 1) * 128],
                                                       lhsT=FP[:, LP, h * 128:(h + 1) * 128], rhs=Lf,
                                                       start=True, stop=True),
                         reads=[B["FP"][LP], B["cst"]], writes=[PB[4]])
                S.op("pe", lambda e: e.matmul(pb[5][:], lhsT=Ur, rhs=FP[:, LP, :], start=True, stop=True),
                     reads=[B["FP"][LP], B["cst"]], writes=[PB[5]])
                S.op("act", lambda e: e.activation(out=FP[:, EQ, :], in_=pb[3][:], func=AF.Exp),
                     reads=[PB[3]], writes=[B["FP"][EQ]])
                S.op("act", lambda e: e.activation(out=FP[:, EK, :], in_=pb[3][:], func=AF.Exp, scale=-1.0),
                     reads=[PB[3]], writes=[B["FP"][EK]])
                S.op("act", lambda e: e.activation(out=FP[:, EB, :], in_=pb[4][:], func=AF.Exp),
                     reads=[PB[4]], writes=[B["FP"][EB]])
                S.op("act", lambda e: e.activation(out=FP[:, EKD, :], in_=pb[5][:], func=AF.Exp),
                     reads=[PB[5]], writes=[B["FP"][EKD]])
                q3 = U1[:, 0:4, cs]
                k3 = U1[:, 4:8, cs]

                def v3(i):
                    return FP[:, i, :].rearrange("p (h t) -> p h t", h=4)

                def h3(i):
                    return HP[:, i, :].rearrange("p (h t) -> p h t", h=4)

                S.op("dve", lambda e, q3=q3: e.tensor_tensor(out=h3(QS), in0=q3, in1=v3(EQ), op=ALU.mult),
                     reads=B["U1"][0:4] + [B["FP"][EQ]], writes=[B["HP"][QS]])
                S.op("dve", lambda e, k3=k3: e.tensor_tensor(out=h3(KS), in0=k3, in1=v3(EK), op=ALU.mult),
                     reads=B["U1"][4:8] + [B["FP"][EK]], writes=[B["HP"][KS]])
                S.op("dve", lambda e, q3=q3: e.tensor_tensor(out=h3(QB), in0=q3, in1=v3(EB), op=ALU.mult),
                     reads=B["U1"][0:4] + [B["FP"][EB]], writes=[B["HP"][QB]])
                fill("B", s)
                for h in range(4):
                    S.op("pe", lambda e, h=h, cs=cs: e.transpose(out=pb[2][:, h * 128:(h + 1) * 128],
                                                                 in_=U1[:, 4 + h, cs], identity=identf),
                         reads=[B["U1"][4 + h], B["cst"]], writes=[PB[2]])
                S.op("dve", lambda e: e.tensor_tensor(out=HP[:, KD, :], in0=pb[2][:], in1=FP[:, EKD, :], op=ALU.mult),
                     reads=[PB[2], B["FP"][EKD]], writes=[B["HP"][KD]])
                for h in range(4):
                    S.op("pe", lambda e, h=h: e.matmul(pb[3][:, h * 128:(h + 1) * 128],
                                                       lhsT=HP[:, KS, h * 128:(h + 1) * 128],
                                                       rhs=HP[:, QS, h * 128:(h + 1) * 128], start=True, stop=True),
                         reads=[B["HP"][KS], B["HP"][QS]], writes=[PB[3]])
                S.op("dve", lambda e: e.tensor_tensor(out=HP[:, ATM, :], in0=pb[3][:], in1=maskT, op=ALU.mult),
                     reads=[PB[3], B["cst"]], writes=[B["HP"][ATM]])
                fill("C", s)
                for h in range(4):
                    for c2 in range(2):
                        cc = h * 2 + c2
                        ob = 4 + cc // 4
                        oc = (cc % 4) * 128
                        S.op("pe", lambda e, h=h, c2=c2, ob=ob, oc=oc, s=s: e.matmul(
                            pb[ob][:, oc:oc + 128],
                            lhsT=vT[:, s, h * 256 + c2 * 128:h * 256 + (c2 + 1) * 128],
                            rhs=HP[:, ATM, h * 128:(h + 1) * 128], start=True, stop=False),
                            reads=[B["vT"][s], B["HP"][ATM]], writes=[PB[ob]])
                        S.op("pe", lambda e, h=h, c2=c2, ob=ob, oc=oc, sbr=sbr: e.matmul(
                            pb[ob][:, oc:oc + 128],
                            lhsT=Sbf[:, sbr, h, c2 * 128:(c2 + 1) * 128],
                            rhs=HP[:, QB, h * 128:(h + 1) * 128], start=False, stop=True),
                            reads=[B["Sbf"][sbr], B["HP"][QB]], writes=[PB[ob]])
                for h in range(4):
                    kb = 2 + h // 2
                    kc0 = (h % 2) * 256
                    S.op("pe", lambda e, h=h, kb=kb, kc0=kc0, s=s: e.matmul(
                        pb[kb][:, kc0:kc0 + 256], lhsT=HP[:, KD, h * 128:(h + 1) * 128],
                        rhs=vT[:, s, h * 256:(h + 1) * 256], start=True, stop=True),
                        reads=[B["HP"][KD], B["vT"][s]], writes=[PB[kb]])
                for h in range(4):
                    kb = 2 + h // 2
                    kc0 = (h % 2) * 256
                    S.op("dve", lambda e, h=h, kb=kb, kc0=kc0: e.scalar_tensor_tensor(
                        out=Sst[:, h, :], in0=Sst[:, h, :], scalar=FP[:, EB, h * 128 + 127:h * 128 + 128],
                        in1=pb[kb][:, kc0:kc0 + 256], op0=ALU.mult, op1=ALU.add),
                        reads=[B["Sst"], B["FP"][EB], PB[kb]], writes=[B["Sst"]])
                S.op("pool", lambda e, sbw=sbw: e.tensor_copy(out=Sbf[:, sbw, :, :], in_=Sst[:]),
                     reads=[B["Sst"]], writes=[B["Sbf"][sbw]])
                for ob in range(2):
                    S.op("act", lambda e, ob=ob: e.activation(out=FP[:, OS0 + ob, :], in_=pb[4 + ob][:], func=AF.Copy),
                         reads=[PB[4 + ob]], writes=[B["FP"][OS0 + ob]])
                    S.op("act", lambda e, ob=ob: e.activation(out=HP[:, OQ0 + ob, :], in_=pb[4 + ob][:], func=AF.Square),
                         reads=[PB[4 + ob]], writes=[B["HP"][OQ0 + ob]])
                fill("D", s)
                for h in range(4):
                    for c2 in range(2):
                        cc = h * 2 + c2
                        S.op("pe", lambda e, h=h, c2=c2, cc=cc: e.matmul(
                            pb[4][:, h * 128:(h + 1) * 128], lhsT=onesb[:],
                            rhs=HP[:, OQ0 + cc // 4, (cc % 4) * 128:(cc % 4 + 1) * 128],
                            start=(c2 == 0), stop=(c2 == 1)),
                            reads=[B["onesb"], B["HP"][OQ0 + cc // 4]], writes=[PB[4]])
                S.op("act", lambda e: e.activation(out=FP[:, RO, :], in_=pb[4][:], func=AF.Ln, scale=1.0 / 256,
                                                   bias=pp[:, 27:28]),
                     reads=[PB[4], B["pp"]], writes=[B["FP"][RO]])
                S.op("act", lambda e: e.activation(out=FP[:, RO, :], in_=FP[:, RO, :], func=AF.Exp, scale=-0.5),
                     reads=[B["FP"][RO]], writes=[B["FP"][RO]])
                osv = FP[:, OS0:OS0 + 2, :].rearrange("p a (b c t) -> p (a b) c t", b=2, c=2)
                tnv4 = FP[:, TN0:TN0 + 2, :].rearrange("p a (b c t) -> p (a b) c t", b=2, c=2)
                tnv = FP[:, TN0:TN0 + 2, :].rearrange("p a (b t) -> p (a b) t", b=4)
                rov = FP[:, RO, :].rearrange("p (h t) -> p h t", h=4)
                for c2 in range(2):
                    S.op("dve", lambda e, c2=c2, osv=osv, tnv4=tnv4, rov=rov: e.scalar_tensor_tensor(
                        out=tnv4[:, :, c2, :], in0=osv[:, :, c2, :], scalar=pp[:, 24 + c2:25 + c2], in1=rov,
                        op0=ALU.mult, op1=ALU.mult),
                        reads=[B["FP"][OS0], B["FP"][OS1], B["FP"][RO], B["pp"]],
                        writes=[B["FP"][TN0], B["FP"][TN1]])
                S.op("dve", lambda e, cs=cs, tnv=tnv: e.tensor_tensor(out=agla[:, :, cs], in0=tnv, in1=sgr[:, :, cs],
                                                                      op=ALU.mult),
                     reads=[B["FP"][TN0], B["FP"][TN1]] + B["sgr"], writes=B["agla"])

            while mc_next[0] < 8:
                mc_raw()

            set_pbanks([0, 1, 2, 3])
            for c in range(8):
                g1, t1, g2 = c % 2, 2 + c % 2, 4 + c % 2
                S.op("act", lambda e, c=c, g1=g1: e.activation(out=FP[:, g1, :], in_=yb[:, c, :], func=AF.Sigmoid),
                     reads=[B["yb"][c]], writes=[B["FP"][g1]])
                by = proj_from(t, U_WCO(c), aconv, B["aconv"])
                S.op("dve", lambda e, by=by, g1=g1, t1=t1: e.tensor_tensor(out=FP[:, t1, :], in0=pb[by][:],
                                                                           in1=FP[:, g1, :], op=ALU.mult),
                     reads=[PB[by], B["FP"][g1]], writes=[B["FP"][t1]])
                bm2 = proj_fm(t, U_MG(c), hbuf)
                S.op("act", lambda e, bm2=bm2, g2=g2: e.activation(out=FP[:, g2, :], in_=pb[bm2][:], func=AF.Sigmoid),
                     reads=[PB[bm2]], writes=[B["FP"][g2]])
                by2 = proj_from(t, U_WGO(c), agla, B["agla"])
                S.op("dve", lambda e, by2=by2, g2=g2: e.tensor_tensor(out=FP[:, g2, :], in0=pb[by2][:],
                                                                      in1=FP[:, g2, :], op=ALU.mult),
                     reads=[PB[by2], B["FP"][g2]], writes=[B["FP"][g2]])
                S.op("pool", lambda e, c=c, t1=t1, g2=g2: e.tensor_tensor(out=yb[:, c, :], in0=FP[:, t1, :],
                                                                          in1=FP[:, g2, :], op=ALU.add),
                     reads=[B["FP"][t1], B["FP"][g2]], writes=[B["yb"][c]])

            wo = [wide(t, 2), None]
            wo[1] = wide(t, 3)
            for s in range(4):
                f = cnt["fin"]
                cnt["fin"] += 1
                p = f % 2
                xr0 = 6 + p * 2
                ob0 = p * 2
                r0 = t * T + s * 128
                xrv = FP[:, xr0:xr0 + 2, :].rearrange("p a t -> p (a t)")
                obv = FP[:, ob0:ob0 + 2, :].rearrange("p a t -> p (a t)")
                S.op("act", lambda e, xrv=xrv, r0=r0: e.dma_start(out=xrv, in_=x_d[r0:r0 + 128, :]),
                     writes=[B["FP"][xr0], B["FP"][xr0 + 1]], dma_sem=xr_sems[p])
                for half in range(2):
                    wap, wbuf = wo[half]
                    bi = next_pbank()
                    for kc in range(8):
                        S.op("pe", lambda e, bi=bi, kc=kc, s=s, wap=wap: e.matmul(
                            pb[bi][:], lhsT=yb[:, kc, s * 128:(s + 1) * 128],
                            rhs=wap[:, kc * 512:(kc + 1) * 512], start=(kc == 0), stop=(kc == 7)),
                            reads=[wbuf] + B["yb"], writes=[PB[bi]])
                    S.op("dve", lambda e, bi=bi, half=half, xr0=xr0: e.tensor_tensor(
                        out=FP[:, xr0 + half, :], in0=pb[bi][:], in1=FP[:, xr0 + half, :], op=ALU.add),
                        reads=[PB[bi], B["FP"][xr0 + half]], writes=[B["FP"][xr0 + half]])
                col = 8 + (f % 8)
                jv = HP[:, 0:2, :].rearrange("p a t -> p (a t)")
                S.op("act", lambda e, xrv=xrv, jv=jv, col=col: e.activation(
                    out=jv, in_=xrv, func=AF.Square, accum_out=ssx[:, col:col + 1]),
                    reads=[B["FP"][xr0], B["FP"][xr0 + 1]], writes=[B["HP"][0], B["HP"][1], B["ssx"][col]])
                S.op("pool", lambda e, col=col: e.tensor_scalar(
                    out=ssx[:, col:col + 1], in0=ssx[:, col:col + 1], scalar1=1.0 / D, scalar2=EPS,
                    op0=ALU.mult, op1=ALU.add), reads=[B["ssx"][col]], writes=[B["ssx"][col]])
                S.op("pool", lambda e, col=col: e.tensor_tensor(
                    out=ssx[:, col:col + 1], in0=ssx[:, col:col + 1], in1=pp[:, 26:27], op=ALU.pow),
                    reads=[B["ssx"][col], B["pp"]], writes=[B["ssx"][col]])
                S.op("dve", lambda e, xrv=xrv, obv=obv, col=col: e.scalar_tensor_tensor(
                    out=obv, in0=xrv, scalar=ssx[:, col:col + 1], in1=Gf[:], op0=ALU.mult, op1=ALU.mult),
                    reads=[B["FP"][xr0], B["FP"][xr0 + 1], B["ssx"][col], B["Gf"]],
                    writes=[B["FP"][ob0], B["FP"][ob0 + 1]])
                S.op("sp", lambda e, obv=obv, r0=r0: e.dma_start(out=out_d[r0:r0 + 128, :], in_=obv),
                     reads=[B["FP"][ob0], B["FP"][ob0 + 1]], writes=[B["outdone"][p]], dma_sem=out_sems[p])

        for s_ in range(4):
            pre_elem(0, s_)
        issue_casts()
        for s_ in range(4):
            pre_pe(0, s_)
        for t in range(n_tiles):
            tile(t)
        S.op("sp", None, reads=B["outdone"], writes=B["outdone"])
        S.finalize_counts()

        dma_sems = {n: es.enter_context(nc.semaphore(n)) for n in dma_sem_names}
        with nc.Block() as block:
            @block.tensor
            def _(e):
                S.replay("pe", e, eng_sems, dma_sems)

            @block.scalar
            def _(e):
                S.replay("act", e, eng_sems, dma_sems)

            @block.vector
            def _(e):
                S.replay("dve", e, eng_sems, dma_sems)

            @block.gpsimd
            def _(e):
                S.replay("pool", e, eng_sems, dma_sems)

            @block.sync
            def _(e):
                S.replay("sp", e, eng_sems, dma_sems)
    return nc, S


def _unit(W, col0, ncols, width):
    blk = np.zeros((1024, width), np.float32)
    blk[:, :ncols] = W[:, col0:col0 + ncols]
    return blk.reshape(8, 128, width).transpose(1, 0, 2).reshape(128, 8 * width)


def _pack_weights(w_in, w_conv_out, w_gla_out, w_out):
    nar = np.zeros((N_NAR, 128, 1024), np.float32)
    wide = np.zeros((N_WIDE, 128, 4096), np.float32)
    o_cval, o_cgate, o_cz, o_q, o_k, o_v, o_glr, o_gr, o_mc, o_mg = 0, 1024, 2048, 3072, 3584, 4096, 5120, 5136, 6160, 7184
    nar[U_GLR] = _unit(w_in, o_glr, 16, 128)
    for c in range(8):
        nar[U_CG(c)] = _unit(w_in, o_cgate + c * 128, 128, 128)
        nar[U_CV(c)] = _unit(w_in, o_cval + c * 128, 128, 128)
        nar[U_CZ(c)] = _unit(w_in, o_cz + c * 128, 128, 128)
        nar[U_GR(c)] = _unit(w_in, o_gr + c * 128, 128, 128)
        nar[U_MC(c)] = _unit(w_in, o_mc + c * 128, 128, 128)
        nar[U_MG(c)] = _unit(w_in, o_mg + c * 128, 128, 128)
        nar[U_WCO(c)] = _unit(w_conv_out, c * 128, 128, 128)
        nar[U_WGO(c)] = _unit(w_gla_out, c * 128, 128, 128)
    for h in range(4):
        nar[U_Q(h)] = _unit(w_in, o_q + h * 128, 128, 128)
        nar[U_K(h)] = _unit(w_in, o_k + h * 128, 128, 128)
    wide[0] = _unit(w_in, o_v, 512, 512)
    wide[1] = _unit(w_in, o_v + 512, 512, 512)
    wide[2] = _unit(w_out, 0, 512, 512)
    wide[3] = _unit(w_out, 512, 512, 512)
    return nar, wide


def _const_tables():
    j = np.arange(128)[:, None]
    i = np.arange(128)[None, :]
    ident = (j == i).astype(np.float32)
    le = (j <= i).astype(np.float32)
    Lf = -le / 16.0
    Lr = -(le - (j <= 63).astype(np.float32)) / 16.0
    Ur = -(j > i).astype(np.float32) / 16.0
    maskT = np.tile(le, (1, 4))
    p = np.arange(128)[:, None]
    istack = np.tile(((p % 32) == np.arange(32)[None, :]).astype(np.float32), (1, 8))
    return np.concatenate([ident, Lf, Lr, Ur, maskT, istack], axis=1).astype(np.float32)


def _prep_inputs(x, norm_g, w_in, conv_w, conv_b, conv_ln_g, conv_ln_b, w_conv_out,
                 gate_w2, gate_b, gla_norm_g, w_gla_out, w_out, final_g):
    f = lambda a: np.ascontiguousarray(np.asarray(a, dtype=np.float32))
    x = f(x)
    nar, wide = _pack_weights(f(w_in)[0], f(w_conv_out)[0], f(w_gla_out)[0], f(w_out)[0])
    pp = np.zeros((128, 32), np.float32)
    pp[:, 0:8] = f(conv_b)[0].reshape(8, 128).T
    pp[:, 8:16] = f(conv_ln_g)[0].reshape(8, 128).T
    pp[:, 16:24] = f(conv_ln_b)[0].reshape(8, 128).T
    pp[:, 24:26] = f(gla_norm_g)[0].reshape(2, 128).T
    pp[:, 26] = -0.5
    pp[:, 27] = EPS
    cwp = np.zeros((32, D), np.float32)
    cwp[:KTAPS] = f(conv_w)[0]
    cw = np.ascontiguousarray(cwp.reshape(8, 4, 8, 4, 32).transpose(1, 4, 2, 3, 0)).reshape(128, 8, 4, 8)
    gw2b = np.zeros((32, 512), np.float32)
    gw2b[0:16] = f(gate_w2)[0]
    gw2b[16] = f(gate_b)[0]
    shared = {
        "wnar": nar, "wwide": wide, "pp": pp, "cw": cw.reshape(128, 256), "gw2b": gw2b,
        "ng": f(norm_g)[0], "fg": f(final_g), "cst": _const_tables(),
    }
    xs = x.reshape(NCORE, TOK_CORE, D)
    return [dict(shared, x=np.ascontiguousarray(xs[c])) for c in range(NCORE)]


def kernel(x, norm_g, w_in, conv_w, conv_b, conv_ln_g, conv_ln_b, w_conv_out,
           gate_w2, gate_b, gla_norm_g, w_gla_out, w_out, final_g):
    in_maps = _prep_inputs(x, norm_g, w_in, conv_w, conv_b, conv_ln_g, conv_ln_b, w_conv_out,
                           gate_w2, gate_b, gla_norm_g, w_gla_out, w_out, final_g)
    nc, _ = build_nc()
    res = run_bass_kernel_spmd(nc, in_maps, core_ids=list(range(NCORE)))
    out = np.stack([np.asarray(r["out"], dtype=np.float32) for r in res.results], axis=0)
    return out.reshape(16, SEQ, D)
```

```python
import numpy as np
from contextlib import ExitStack
import concourse.bass as bass
import concourse.mybir as mybir
from concourse.bass_utils import run_bass_kernel_spmd

F32 = mybir.dt.float32
BF16 = mybir.dt.bfloat16
AF = mybir.ActivationFunctionType
ALU = mybir.AluOpType

ENGS = ("pe", "act", "dve", "pool", "sp")

D = 1024
SEQ = 2048
NCORE = 8
TOK_CORE = 4096
T = 512
NT = TOK_CORE // T
TPS = SEQ // T
EPS = 1e-6
NR = 10
KTAPS = 31

U_GLR = 0
U_CG = lambda c: 1 + 2 * c
U_CV = lambda c: 2 + 2 * c
U_CZ = lambda c: 17 + c
U_GR = lambda c: 25 + c
U_Q = lambda h: 33 + h
U_K = lambda h: 37 + h
U_MC = lambda c: 41 + c
U_WCO = lambda c: 49 + 3 * c
U_MG = lambda c: 50 + 3 * c
U_WGO = lambda c: 51 + 3 * c
N_NAR = 73
N_WIDE = 4


class Buf:
    __slots__ = ("name", "w", "r")

    def __init__(self, name):
        self.name = name
        self.w = None
        self.r = {}


class Op:
    __slots__ = ("eng", "fn", "waits", "marked", "clock", "idx", "dma_sem", "dma_val")


class Sched:
    def __init__(self):
        self.prog = {e: [] for e in ENGS}
        self.clock = {e: {} for e in ENGS}
        self.dma_cnt = {}
        self._dma_clock = {}
        self.n_waits = 0

    def op(self, eng, fn, reads=(), writes=(), dma_sem=None):
        is_dma = dma_sem is not None
        deps = {}

        def add(tok, raw):
            if tok is None:
                return
            if tok[0] == "e" and tok[1] == eng and not is_dma and not raw and eng == "pe":
                return
            k = (tok[0], tok[1])
            if deps.get(k, -1) < tok[2]:
                deps[k] = tok[2]

        for b in reads:
            add(b.w, True)
        for b in writes:
            add(b.w, False)
            for kk, vv in b.r.items():
                add((kk[0], kk[1], vv), False)
        clk = self.clock[eng]
        o = Op()
        o.eng = eng
        o.fn = fn
        o.waits = []
        o.marked = False
        o.dma_sem = dma_sem
        o.idx = len(self.prog[eng])
        for k, v in deps.items():
            if clk.get(k, -1) >= v:
                continue
            o.waits.append((k, v))
            self.n_waits += 1
            if k[0] == "e":
                src = self.prog[k[1]][v]
                src.marked = True
                src_clock = src.clock
            else:
                src_clock = self._dma_clock[(k[1], v)]
            for kk, vv in src_clock.items():
                if clk.get(kk, -1) < vv:
                    clk[kk] = vv
            if clk.get(k, -1) < v:
                clk[k] = v
        if is_dma:
            val = self.dma_cnt.get(dma_sem, 0) + 16
            self.dma_cnt[dma_sem] = val
            o.dma_val = val
            tok = ("d", dma_sem, val)
            self._dma_clock[(dma_sem, val)] = dict(clk)
        else:
            tok = ("e", eng, o.idx)
        o.clock = dict(clk)
        self.prog[eng].append(o)
        for b in reads:
            b.r[(tok[0], tok[1])] = tok[2]
        for b in writes:
            b.w = tok
            b.r = {}
        return tok

    def finalize_counts(self):
        self.cnts = {}
        for e in ENGS:
            c = 0
            lst = []
            for o in self.prog[e]:
                if o.marked:
                    c += 1
                lst.append(c)
            self.cnts[e] = lst

    def replay(self, eng, engobj, eng_sems, dma_sems):
        for o in self.prog[eng]:
            for k, v in o.waits:
                if k[0] == "e":
                    engobj.wait_ge(eng_sems[k[1]], self.cnts[k[1]][v])
                else:
                    engobj.wait_ge(dma_sems[k[1]], v)
            if o.fn is None:
                continue
            inst = o.fn(engobj)
            if o.dma_sem is not None:
                inst.then_inc(dma_sems[o.dma_sem], 16)
            elif o.marked:
                inst.then_inc(eng_sems[eng], 1)


def build_nc(n_tiles=NT):
    nc = bass.Bass("TRN2", target_bir_lowering=False)
    x_d = nc.dram_tensor("x", [TOK_CORE, D], F32, kind="ExternalInput").ap()
    wnar_f = nc.dram_tensor("wnar", [N_NAR, 128, 1024], F32, kind="ExternalInput").ap()
    wwide_f = nc.dram_tensor("wwide", [N_WIDE, 128, 4096], F32, kind="ExternalInput").ap()
    pp_d = nc.dram_tensor("pp", [128, 32], F32, kind="ExternalInput").ap()
    cw_d = nc.dram_tensor("cw", [128, 8 * 4 * 8], F32, kind="ExternalInput").ap()
    gw2b_d = nc.dram_tensor("gw2b", [32, 512], F32, kind="ExternalInput").ap()
    ng_d = nc.dram_tensor("ng", [D], F32, kind="ExternalInput").ap()
    fg_d = nc.dram_tensor("fg", [D], F32, kind="ExternalInput").ap()
    cst_d = nc.dram_tensor("cst", [128, 4 * 128 + 512 + 256], F32, kind="ExternalInput").ap()
    out_d = nc.dram_tensor("out", [TOK_CORE, D], F32, kind="ExternalOutput").ap()
    wnar_b = nc.dram_tensor("wnar_b", [N_NAR, 128, 1024], BF16, kind="Internal").ap()
    wwide_b = nc.dram_tensor("wwide_b", [N_WIDE, 128, 4096], BF16, kind="Internal").ap()
    usc_t = nc.dram_tensor("usc", [4, 128, 544], BF16, kind="Internal")
    usc = usc_t.ap()

    S = Sched()
    dma_sem_names = []

    def newsem(name):
        dma_sem_names.append(name)
        return name

    with ExitStack() as es:
        def sb(name, shape, dt):
            return es.enter_context(nc.sbuf_tensor("sb_" + name, shape, dt))

        def ps(name, shape, dt):
            return es.enter_context(nc.psum_tensor("ps_" + name, shape, dt))

        cst = sb("cst", [128, 4 * 128 + 512 + 256], F32)
        identf = cst[:, 0:128]
        Lf = cst[:, 128:256]
        Lr = cst[:, 256:384]
        Ur = cst[:, 384:512]
        maskT = cst[:, 512:1024]
        identb = sb("identb", [128, 128], BF16)
        identstack = sb("identstack", [128, 8, 32], BF16)
        onesb = sb("onesb", [128, 128], BF16)
        Gn = sb("Gn", [128, D], F32)
        Gf = sb("Gf", [128, D], F32)
        pp = sb("pp", [128, 32], F32)
        cw = sb("cw", [128, 8, 4, 8], F32)
        gw2b = sb("gw2b", [32, 512], BF16)
        glr = sb("glr", [32, 512], BF16)
        wn = sb("wn", [128, NR, 1024], BF16)
        ww = sb("ww", [128, 2, 4096], BF16)
        W4 = sb("W4", [128, 4, 4, 8, 32], BF16)
        xa = sb("xa", [128, 2, D], F32)
        hb = sb("hb", [128, 4, D], BF16)
        hT = sb("hT", [128, 2, 8, T], BF16)
        u = sb("u", [128, 4, 544], BF16)
        u4 = sb("u4", [128, 4, 4, 544], BF16)
        halo = sb("halo", [128, 8, 32], BF16)
        U1 = sb("U1", [128, 8, T], F32)
        aconv = sb("aconv", [128, 8, T], BF16)
        sgr = sb("sgr", [128, 8, T], BF16)
        vT = sb("vT", [128, 4, D], BF16)
        agla = sb("agla", [128, 8, T], BF16)
        yb = sb("yb", [128, 8, T], BF16)
        Sst = sb("Sst", [128, 4, 256], F32)
        Sbf = sb("Sbf", [128, 2, 4, 256], BF16)
        ssx = sb("ssx", [128, 16], F32)
        FP = sb("FP", [128, 10, 512], F32)
        HP = sb("HP", [128, 8, 512], BF16)
        pb = [ps("pb%d" % i, [128, 512], F32) for i in range(6)]
        pb6 = ps("pb6", [128, 1024], BF16)
        pb7 = ps("pb7", [128, 512], F32)
        pb = pb + [pb6, pb7]

        eng_sems = {e: es.enter_context(nc.semaphore("s_" + e)) for e in ENGS}

        B = {}

        def mk(name, n=None):
            if n is None:
                B[name] = Buf(name)
            else:
                B[name] = [Buf("%s%d" % (name, i)) for i in range(n)]

        for nm in ["cst", "identb", "identstack", "onesb", "Gn", "Gf", "pp", "cw", "gw2b", "glr",
                   "halo", "Sst"]:
            mk(nm)
        mk("ssx", 16)
        mk("wn", NR); mk("ww", 2); mk("W4", 4); mk("xa", 2); mk("hb", 4); mk("hT", 2); mk("u", 4)
        mk("U1", 8); mk("aconv", 8); mk("sgr", 8); mk("vT", 4); mk("agla", 8); mk("yb", 8)
        mk("Sbf", 2); mk("FP", 10); mk("HP", 8); mk("pb", 8)
        mk("wnar_b", N_NAR); mk("wwide_b", N_WIDE)
        mk("outdone", 2)
        B["u4"] = [[Buf("u4_%d_%d" % (i, g)) for g in range(4)] for i in range(4)]
        mk("usc", 4)

        PB = B["pb"]

        def pbank(i):
            if i < 6:
                return pb[i]
            return pb6 if i == 6 else pb7

        S.op("sp", lambda e: e.dma_start(out=cst[:], in_=cst_d), writes=[B["cst"]], dma_sem=newsem("c_cst"))
        S.op("sp", lambda e: e.dma_start(out=pp[:], in_=pp_d), writes=[B["pp"]], dma_sem=newsem("c_pp"))
        S.op("sp", lambda e: e.dma_start(out=cw[:].rearrange("p a b c -> p (a b c)"), in_=cw_d), writes=[B["cw"]],
             dma_sem=newsem("c_cw"))
        S.op("sp", lambda e: e.dma_start(out=Gn[:], in_=ng_d.partition_broadcast(128)), writes=[B["Gn"]],
             dma_sem=newsem("c_gn"))
        S.op("sp", lambda e: e.dma_start(out=Gf[:], in_=fg_d.partition_broadcast(128)), writes=[B["Gf"]],
             dma_sem=newsem("c_gf"))
        S.op("pool", lambda e: e.dma_start(out=gw2b[:], in_=gw2b_d), writes=[B["gw2b"]], dma_sem=newsem("c_gw"))

        nar_groups = [(0, 1), (1, 3), (3, 7), (7, 11), (11, 17), (17, 21), (21, 25), (25, 29), (29, 33)]
        wide_first = [(0, 1), (1, 2)]
        nar_groups2 = [(33, 37), (37, 41), (41, 45), (45, 49), (49, 53), (53, 57), (57, 61), (61, 65),
                       (65, 69), (69, 73)]
        wide_second = [(2, 3), (3, 4)]

        def cast_nar(a, b_):
            S.op("pool", lambda e: e.dma_start(out=wnar_b[a:b_], in_=wnar_f[a:b_]),
                 writes=B["wnar_b"][a:b_], dma_sem=newsem("cn%d" % a))

        def cast_wide(a, b_):
            S.op("pool", lambda e: e.dma_start(out=wwide_b[a:b_], in_=wwide_f[a:b_]),
                 writes=B["wwide_b"][a:b_], dma_sem=newsem("cw%d" % a))

        cast_queue = ([("n", a, b_) for (a, b_) in nar_groups] + [("w", a, b_) for (a, b_) in wide_first]
                      + [("n", a, b_) for (a, b_) in nar_groups2] + [("w", a, b_) for (a, b_) in wide_second])
        cast_pos = [0]

        def issue_casts(n_units_nar=None, wide_upto=None, everything=False):
            while cast_pos[0] < len(cast_queue):
                kind, a, b_ = cast_queue[cast_pos[0]]
                if not everything:
                    if kind == "n" and (n_units_nar is None or a >= n_units_nar):
                        if kind == "n":
                            break
                    if kind == "w" and (wide_upto is None or a >= wide_upto):
                        break
                if kind == "n":
                    cast_nar(a, b_)
                else:
                    cast_wide(a, b_)
                cast_pos[0] += 1

        S.op("dve", lambda e: e.tensor_copy(out=identb[:], in_=identf), reads=[B["cst"]], writes=[B["identb"]])
        S.op("dve", lambda e: e.tensor_copy(out=identstack[:].rearrange("p a b -> p (a b)"), in_=cst[:, 1024:1280]),
             reads=[B["cst"]], writes=[B["identstack"]])
        S.op("pool", lambda e: e.memset(u[:], 0.0), writes=B["u"])
        S.op("pool", lambda e: e.memset(onesb[:], 1.0), writes=[B["onesb"]])
        S.op("pool", lambda e: e.memset(glr[:], 1.0), writes=[B["glr"]])

        nar_sems = [newsem("wn%d" % i) for i in range(NR)]
        wide_sems = [newsem("ww%d" % i) for i in range(2)]
        nar_loaded = [0]
        wide_loaded = [0]
        n_nar_total = n_tiles * N_NAR
        n_wide_total = n_tiles * N_WIDE

        def nar_ensure(g):
            lim = min(g + NR, n_nar_total)
            while nar_loaded[0] < lim:
                m = nar_loaded[0]
                slot = m % NR
                uid = m % N_NAR
                assert B["wnar_b"][uid].w is not None, ("cast not issued", uid)
                S.op("sp", lambda e, slot=slot, uid=uid: e.dma_start(out=wn[:, slot, :], in_=wnar_b[uid]),
                     reads=[B["wnar_b"][uid]], writes=[B["wn"][slot]], dma_sem=nar_sems[slot])
                nar_loaded[0] += 1

        def wide_ensure(lim):
            lim = min(lim, n_wide_total)
            while wide_loaded[0] < lim:
                m = wide_loaded[0]
                slot = m % 2
                uid = m % N_WIDE
                assert B["wwide_b"][uid].w is not None, ("wide cast not issued", uid)
                S.op("sp", lambda e, slot=slot, uid=uid: e.dma_start(out=ww[:, slot, :], in_=wwide_b[uid]),
                     reads=[B["wwide_b"][uid]], writes=[B["ww"][slot]], dma_sem=wide_sems[slot])
                wide_loaded[0] += 1

        nar_last = [-1]

        def nar(t, uid):
            g = t * N_NAR + uid
            assert g == nar_last[0] + 1, (g, nar_last[0])
            nar_last[0] = g
            nar_ensure(g)
            slot = g % NR
            return wn[:, slot, :], B["wn"][slot]

        def wide(t, uid):
            g = t * N_WIDE + uid
            wide_ensure(g + 1)
            slot = g % 2
            return ww[:, slot, :], B["ww"][slot]

        pj = {"set": [0, 1], "i": 0}

        def set_pbanks(banks):
            pj["set"] = list(banks)
            pj["i"] = 0

        def next_pbank():
            pj["i"] = (pj["i"] + 1) % len(pj["set"])
            return pj["set"][pj["i"]]

        def proj_fm(t, uid, hbuf, coltile=False):
            wap, wbuf = nar(t, uid)
            bi = next_pbank()
            for kc in range(8):
                if not coltile:
                    S.op("pe", lambda e, bi=bi, kc=kc, wap=wap: e.matmul(
                        pb[bi][:], lhsT=wap[:, kc * 128:(kc + 1) * 128], rhs=hT[:, hbuf, kc, :],
                        start=(kc == 0), stop=(kc == 7)),
                        reads=[wbuf, B["hT"][hbuf]], writes=[PB[bi]])
                else:
                    for b in range(4):
                        S.op("pe", lambda e, bi=bi, kc=kc, wap=wap, b=b: e.matmul(
                            pb[bi][32 * b:32 * b + 32, :],
                            lhsT=wap[:, kc * 128 + 32 * b:kc * 128 + 32 * b + 32], rhs=hT[:, hbuf, kc, :],
                            start=(kc == 0), stop=(kc == 7), tile_position=(0, 32 * b)),
                            reads=[wbuf, B["hT"][hbuf]], writes=[PB[bi]])
            return bi

        def proj_from(t, uid, src, srcbufs):
            wap, wbuf = nar(t, uid)
            bi = next_pbank()
            for kc in range(8):
                S.op("pe", lambda e, bi=bi, kc=kc, wap=wap: e.matmul(
                    pb[bi][:], lhsT=wap[:, kc * 128:(kc + 1) * 128], rhs=src[:, kc, :],
                    start=(kc == 0), stop=(kc == 7)),
                    reads=[wbuf] + list(srcbufs), writes=[PB[bi]])
            return bi

        xa_sems = [newsem("xa0"), newsem("xa1")]
        pre_cnt = [0]

        def pre_elem(t, s):
            g = t * 4 + s
            p = g % 2
            hs = s
            r0 = t * T + s * 128
            S.op("act", lambda e, p=p, r0=r0: e.dma_start(out=xa[:, p, :], in_=x_d[r0:r0 + 128, :]),
                 writes=[B["xa"][p]], dma_sem=xa_sems[p])
            col = g % 8
            S.op("act", lambda e, p=p, hs=hs, col=col: e.activation(
                out=hb[:, hs, :], in_=xa[:, p, :], func=AF.Square, accum_out=ssx[:, col:col + 1]),
                reads=[B["xa"][p]], writes=[B["hb"][hs], B["ssx"][col]])
            S.op("pool", lambda e, col=col: e.tensor_scalar(
                out=ssx[:, col:col + 1], in0=ssx[:, col:col + 1], scalar1=1.0 / D, scalar2=EPS,
                op0=ALU.mult, op1=ALU.add), reads=[B["ssx"][col]], writes=[B["ssx"][col]])
            S.op("pool", lambda e, col=col: e.tensor_tensor(
                out=ssx[:, col:col + 1], in0=ssx[:, col:col + 1], in1=pp[:, 26:27], op=ALU.pow),
                reads=[B["ssx"][col], B["pp"]], writes=[B["ssx"][col]])
            S.op("dve", lambda e, p=p, hs=hs, col=col: e.scalar_tensor_tensor(
                out=hb[:, hs, :], in0=xa[:, p, :], scalar=ssx[:, col:col + 1], in1=Gn[:],
                op0=ALU.mult, op1=ALU.mult),
                reads=[B["xa"][p], B["ssx"][col], B["Gn"]], writes=[B["hb"][hs]])

        def pre_pe(t, s):
            p = s
            hbuf = t % 2
            for c in range(8):
                S.op("pe", lambda e, p=p, c=c: e.transpose(
                    out=pb6[:, c * 128:(c + 1) * 128], in_=hb[:, p, c * 128:(c + 1) * 128], identity=identb[:]),
                    reads=[B["hb"][p], B["identb"]], writes=[PB[6]])
            S.op("act", lambda e, hbuf=hbuf, s=s: e.activation(
                out=hT[:, hbuf, :, s * 128:(s + 1) * 128],
                in_=pb6[:].rearrange("p (c t) -> p c t", c=8), func=AF.Copy),
                reads=[PB[6]], writes=[B["hT"][hbuf]])

        def preamble_a(t):
            for s in range(4):
                pre_elem(t, s)
                pre_pe(t, s)

        xr_sems = [newsem("xr0"), newsem("xr1")]
        out_sems = [newsem("o0"), newsem("o1")]
        cnt = {"ub": 0, "db": 0, "cb": 0, "sg": 0, "fin": 0}
        u4_sems = [[newsem("u4s_%d_%d" % (i, g)) for g in range(4)] for i in range(4)]
        usc_sems = [newsem("usc%d" % i) for i in range(4)]

        def tile(t):
            hbuf = t % 2
            seq_start = (t % TPS == 0)
            if seq_start:
                S.op("pool", lambda e: e.memset(halo[:], 0.0), writes=[B["halo"]])
                S.op("pool", lambda e: e.memset(Sst[:], 0.0), writes=[B["Sst"]])
                sb0 = (t * 4) % 2
                S.op("pool", lambda e, sb0=sb0: e.memset(Sbf[:, sb0, :, :], 0.0), writes=[B["Sbf"][sb0]])

            bi = proj_fm(t, U_GLR, hbuf)
            S.op("act", lambda e, bi=bi: e.activation(out=glr[0:16, :], in_=pb[bi][0:16, :], func=AF.Copy),
                 reads=[PB[bi]], writes=[B["glr"]])

            set_pbanks([0, 1, 7])
            s2 = {}

            def s2_A(c):
                ub = cnt["ub"] % 4
                cnt["ub"] += 1
                db = cnt["db"] % 4
                cnt["db"] += 1
                s2[c] = (ub, db)
                for b in range(4):
                    S.op("pool", lambda e, db=db, c=c, b=b: e.tensor_tensor(
                        out=W4[:, db, b, :, :], in0=identstack[:],
                        in1=cw[:, c, b, :].unsqueeze(2).to_broadcast([128, 8, 32]), op=ALU.mult),
                        reads=[B["identstack"], B["cw"]], writes=[B["W4"][db]])
                bg = proj_fm(t, U_CG(c), hbuf, coltile=True)
                sgi = c % 2
                S.op("act", lambda e, bg=bg, sgi=sgi: e.activation(out=FP[:, sgi, :], in_=pb[bg][:], func=AF.Sigmoid),
                     reads=[PB[bg]], writes=[B["FP"][sgi]])
                bv = proj_fm(t, U_CV(c), hbuf, coltile=True)
                S.op("pool", lambda e, ub=ub, c=c: e.tensor_copy(out=u[:, ub, 0:30], in_=halo[:, c, 0:30]),
                     reads=[B["halo"]], writes=[B["u"][ub]])
                S.op("dve", lambda e, bv=bv, sgi=sgi, ub=ub: e.tensor_tensor(
                    out=u[:, ub, 30:542], in0=pb[bv][:], in1=FP[:, sgi, :], op=ALU.mult),
                    reads=[PB[bv], B["FP"][sgi]], writes=[B["u"][ub]])
                S.op("pool", lambda e, ub=ub, c=c: e.tensor_copy(out=halo[:, c, 0:30], in_=u[:, ub, 512:542]),
                     reads=[B["u"][ub]], writes=[B["halo"]])
                S.op("sp", lambda e, ub=ub: e.dma_start(out=usc[ub], in_=u[:, ub, :]),
                     reads=[B["u"][ub]], writes=[B["usc"][ub]], dma_sem=usc_sems[ub])

            def s2_R(c):
                ub, db = s2[c]
                for g in range(4):
                    src = bass.AP(usc_t, ub * 128 * 544 + g, [[544, 32], [32 * 544, 4], [1, 540]])
                    S.op("sp", lambda e, ub=ub, g=g, src=src: e.dma_start(
                        out=u4[32 * g:32 * g + 32, ub, :, 0:540], in_=src),
                        reads=[B["usc"][ub]], writes=[B["u4"][ub][g]], dma_sem=u4_sems[ub][g])

            def s2_B(c):
                ub, db = s2[c]
                cbk = 2 + (c % 2)
                for j in range(8):
                    for b in range(4):
                        S.op("pe", lambda e, cbk=cbk, db=db, j=j, b=b, ub=ub: e.matmul(
                            pb[cbk][32 * b:32 * b + 32, :], lhsT=W4[:, db, b, j, :],
                            rhs=u4[:, ub, b, 4 * j:4 * j + 512],
                            start=(j == 0), stop=(j == 7), tile_position=(0, 32 * b)),
                            reads=[B["W4"][db]] + B["u4"][ub], writes=[PB[cbk]])
                S.op("act", lambda e, cbk=cbk, c=c: e.activation(
                    out=U1[:, c, :], in_=pb[cbk][:], func=AF.Identity, bias=pp[:, c:c + 1]),
                    reads=[PB[cbk], B["pp"]], writes=[B["U1"][c]])
                vsq = 2 + c % 2
                S.op("act", lambda e, cbk=cbk, c=c, vsq=vsq: e.activation(
                    out=HP[:, vsq, :], in_=pb[cbk][:], func=AF.Square, bias=pp[:, c:c + 1]),
                    reads=[PB[cbk], B["pp"]], writes=[B["HP"][vsq]])
                vbi = c % 2
                S.op("act", lambda e, cbk=cbk, c=c, vbi=vbi: e.activation(
                    out=HP[:, vbi, :], in_=pb[cbk][:], func=AF.Identity, bias=pp[:, c:c + 1]),
                    reads=[PB[cbk], B["pp"]], writes=[B["HP"][vbi]])

            def s2_C(c):
                vsq = 2 + c % 2
                vbi = c % 2
                for b in range(4):
                    S.op("pe", lambda e, vbi=vbi, c=c, b=b: e.matmul(
                        pb[4][32 * b:32 * b + 32, :], lhsT=onesb[:, 32 * b:32 * b + 32], rhs=HP[:, vbi, :],
                        start=(c == 0), stop=(c == 7), tile_position=(0, 32 * b)),
                        reads=[B["onesb"], B["HP"][vbi]], writes=[PB[4]])
                for b in range(4):
                    S.op("pe", lambda e, vsq=vsq, c=c, b=b: e.matmul(
                        pb[5][32 * b:32 * b + 32, :], lhsT=onesb[:, 32 * b:32 * b + 32], rhs=HP[:, vsq, :],
                        start=(c == 0), stop=(c == 7), tile_position=(0, 32 * b)),
                        reads=[B["onesb"], B["HP"][vsq]], writes=[PB[5]])

            LAG = 3
            S.op("pe", lambda e: e.drain())
            for i in range(8 + LAG + 1):
                if t == 0:
                    if i == 2:
                        issue_casts(n_units_nar=29)
                    if i == 4:
                        issue_casts(n_units_nar=33, wide_upto=2)
                    if i == 6:
                        issue_casts(n_units_nar=41, wide_upto=2)
                    if i == 8:
                        issue_casts(n_units_nar=57, wide_upto=2)
                    if i == 10:
                        issue_casts(everything=True)
                if t + 1 < n_tiles and i == 4:
                    pre_elem(t + 1, 0)
                if t + 1 < n_tiles and i == 6:
                    pre_elem(t + 1, 1)
                if i < 8:
                    s2_A(i)
                if 0 <= i - 1 < 8:
                    s2_R(i - 1)
                if 0 <= i - LAG < 8:
                    s2_B(i - LAG)
                if 0 <= i - LAG - 1 < 8:
                    s2_C(i - LAG - 1)

            S.op("pe", lambda e: e.drain())
            MU, RS = 2, 3
            S.op("dve", lambda e: e.tensor_scalar(out=FP[:, MU, :], in0=pb[4][:], scalar1=1.0 / D, scalar2=0.0,
                                                  op0=ALU.mult, op1=ALU.add),
                 reads=[PB[4]], writes=[B["FP"][MU]])
            S.op("dve", lambda e: e.tensor_tensor(out=FP[:, RS, :], in0=FP[:, MU, :], in1=FP[:, MU, :], op=ALU.mult),
                 reads=[B["FP"][MU]], writes=[B["FP"][RS]])
            S.op("dve", lambda e: e.scalar_tensor_tensor(out=FP[:, RS, :], in0=pb[5][:], scalar=1.0 / D,
                                                         in1=FP[:, RS, :], op0=ALU.mult, op1=ALU.subtract),
                 reads=[PB[5], B["FP"][RS]], writes=[B["FP"][RS]])

            set_pbanks([0, 1, 2, 3])
            wide_ensure(t * N_WIDE + 2)
            for c in range(8):
                if t + 1 < n_tiles:
                    if c == 0:
                        pre_elem(t + 1, 2)
                    if c == 2:
                        pre_elem(t + 1, 3)
                    if c >= 4:
                        pre_pe(t + 1, c - 4)
                bz = proj_fm(t, U_CZ(c), hbuf)
                S.op("act", lambda e, bz=bz, c=c: e.activation(out=aconv[:, c, :], in_=pb[bz][:], func=AF.Silu),
                     reads=[PB[bz]], writes=[B["aconv"][c]])

            for c in range(8):
                bg = proj_fm(t, U_GR(c), hbuf)
                S.op("act", lambda e, bg=bg, c=c: e.activation(out=sgr[:, c, :], in_=pb[bg][:], func=AF.Silu),
                     reads=[PB[bg]], writes=[B["sgr"][c]])

            S.op("act", lambda e: e.activation(out=FP[:, RS, :], in_=FP[:, RS, :], func=AF.Ln, bias=pp[:, 27:28]),
                 reads=[B["FP"][RS], B["pp"]], writes=[B["FP"][RS]])
            S.op("act", lambda e: e.activation(out=FP[:, RS, :], in_=FP[:, RS, :], func=AF.Exp, scale=-0.5),
                 reads=[B["FP"][RS]], writes=[B["FP"][RS]])

            def s4b(c):
                ti = 4 + c % 4
                S.op("pool", lambda e, c=c, ti=ti: e.tensor_tensor(out=FP[:, ti, :], in0=U1[:, c, :], in1=FP[:, MU, :],
                                                                   op=ALU.subtract),
                     reads=[B["U1"][c], B["FP"][MU]], writes=[B["FP"][ti]])
                S.op("dve", lambda e, ti=ti: e.tensor_tensor(out=FP[:, ti, :], in0=FP[:, ti, :], in1=FP[:, RS, :],
                                                             op=ALU.mult),
                     reads=[B["FP"][ti], B["FP"][RS]], writes=[B["FP"][ti]])
                S.op("act", lambda e, c=c, ti=ti: e.activation(out=FP[:, ti, :], in_=FP[:, ti, :], func=AF.Silu,
                                                               scale=pp[:, 8 + c:9 + c], bias=pp[:, 16 + c:17 + c]),
                     reads=[B["FP"][ti], B["pp"]], writes=[B["FP"][ti]])
                S.op("dve", lambda e, c=c, ti=ti: e.tensor_tensor(out=aconv[:, c, :], in0=FP[:, ti, :],
                                                                  in1=aconv[:, c, :], op=ALU.mult),
                     reads=[B["FP"][ti], B["aconv"][c]], writes=[B["aconv"][c]])

            def qk_proj(i):
                if i < 4:
                    h = i
                    bq = proj_fm(t, U_Q(h), hbuf)
                    S.op("act", lambda e, bq=bq, h=h: e.activation(out=U1[:, h, :], in_=pb[bq][:], func=AF.Copy,
                                                                   scale=float(128 ** -0.5)),
                         reads=[PB[bq]], writes=[B["U1"][h]])
                else:
                    h = i - 4
                    bk = proj_fm(t, U_K(h), hbuf)
                    S.op("act", lambda e, bk=bk, h=h: e.activation(out=U1[:, 4 + h, :], in_=pb[bk][:], func=AF.Copy),
                         reads=[PB[bk]], writes=[B["U1"][4 + h]])

            for i in range(8):
                s4b(i)
                if i >= 1:
                    qk_proj(i - 1)
            qk_proj(7)

            LP, EQ, EK, EB, EKD, OS0, OS1, RO, TN0, TN1 = 0, 1, 4, 5, 6, 7, 8, 9, 2, 3
            QS, QB, KS, KD, ATM, OQ0, OQ1 = 0, 1, 2, 3, 4, 5, 6
            set_pbanks([0, 1, 7])
            mc_next = [0]

            def vproj(s_):
                for half in range(2):
                    wap, wbuf = wide(t, half)
                    bi = next_pbank()
                    for kc in range(8):
                        S.op("pe", lambda e, bi=bi, kc=kc, s=s_, wap=wap: e.matmul(
                            pb[bi][:], lhsT=hT[:, hbuf, kc, s * 128:(s + 1) * 128],
                            rhs=wap[:, kc * 512:(kc + 1) * 512], start=(kc == 0), stop=(kc == 7)),
                            reads=[wbuf, B["hT"][hbuf]], writes=[PB[bi]])
                    if half == 0:
                        S.op("dve", lambda e, bi=bi, s=s_, half=half: e.tensor_copy(
                            out=vT[:, s, half * 512:(half + 1) * 512], in_=pb[bi][:]),
                            reads=[PB[bi]], writes=[B["vT"][s_]])
                    else:
                        S.op("act", lambda e, bi=bi, s=s_, half=half: e.activation(
                            out=vT[:, s, half * 512:(half + 1) * 512], in_=pb[bi][:], func=AF.Copy),
                            reads=[PB[bi]], writes=[B["vT"][s_]])

            def mc_raw():
                c = mc_next[0]
                if c >= 8:
                    return
                mc_next[0] += 1
                bm = proj_fm(t, U_MC(c), hbuf)
                S.op("act", lambda e, bm=bm, c=c: e.activation(out=yb[:, c, :], in_=pb[bm][:], func=AF.Copy),
                     reads=[PB[bm]], writes=[B["yb"][c]])

            def fill(point, s_):
                if point == "A":
                    if s_ + 1 < 4:
                        vproj(s_ + 1)
                    else:
                        mc_raw()
                    if s_ == 2:
                        wide_ensure(t * N_WIDE + 4)
                elif point == "B":
                    mc_raw()
                elif point == "C":
                    mc_raw()
                elif point == "D":
                    if s_ >= 2:
                        mc_raw()

            vproj(0)
            for s in range(4):
                g = t * 4 + s
                sbr = g % 2
                sbw = (g + 1) % 2
                cs = slice(s * 128, (s + 1) * 128)
                S.op("pe", lambda e, cs=cs: e.matmul(pb[2][:], lhsT=glr[0:32, cs], rhs=gw2b[0:32, :],
                                                     start=True, stop=True),
                     reads=[B["glr"], B["gw2b"]], writes=[PB[2]])
                S.op("act", lambda e: e.activation(out=FP[:, LP, :], in_=pb[2][:], func=AF.Exp, scale=-1.0),
                     reads=[PB[2]], writes=[B["FP"][LP]])
                S.op("act", lambda e: e.activation(out=FP[:, LP, :], in_=FP[:, LP, :], func=AF.Ln, bias=1.0),
                     reads=[B["FP"][LP]], writes=[B["FP"][LP]])
                fill("A", s)
                for h in range(4):
                    S.op("pe", lambda e, h=h: e.matmul(pb[3][:, h * 128:(h + 1) * 128],
                                                       lhsT=FP[:, LP, h * 128:(h + 1) * 128], rhs=Lr,
                                                       start=True, stop=True),
                         reads=[B["FP"][LP], B["cst"]], writes=[PB[3]])
                for h in range(4):
                    S.op("pe", lambda e, h=h: e.matmul(pb[4][:, h * 128:(h + 1) * 128],
                                                       lhsT=FP[:, LP, h * 128:(h + 1) * 128], rhs=Lf,
                                                       start=True, stop=True),
                         reads=[B["FP"][LP], B["cst"]], writes=[PB[4]])
                S.op("pe", lambda e: e.matmul(pb[5][:], lhsT=Ur, rhs=FP[:, LP, :], start=True, stop=True),
                     reads=[B["FP"][LP], B["cst"]], writes=[PB[5]])
                S.op("act", lambda e: e.activation(out=FP[:, EQ, :], in_=pb[3][:], func=AF.Exp),
                     reads=[PB[3]], writes=[B["FP"][EQ]])
                S.op("act", lambda e: e.activation(out=FP[:, EK, :], in_=pb[3][:], func=AF.Exp, scale=-1.0),
                     reads=[PB[3]], writes=[B["FP"][EK]])
                S.op("act", lambda e: e.activation(out=FP[:, EB, :], in_=pb[4][:], func=AF.Exp),
                     reads=[PB[4]], writes=[B["FP"][EB]])
                S.op("act", lambda e: e.activation(out=FP[:, EKD, :], in_=pb[5][:], func=AF.Exp),
                     reads=[PB[5]], writes=[B["FP"][EKD]])
                q3 = U1[:, 0:4, cs]
                k3 = U1[:, 4:8, cs]

                def v3(i):
                    return FP[:, i, :].rearrange("p (h t) -> p h t", h=4)

                def h3(i):
                    return HP[:, i, :].rearrange("p (h t) -> p h t", h=4)

                S.op("dve", lambda e, q3=q3: e.tensor_tensor(out=h3(QS), in0=q3, in1=v3(EQ), op=ALU.mult),
                     reads=B["U1"][0:4] + [B["FP"][EQ]], writes=[B["HP"][QS]])
                S.op("dve", lambda e, k3=k3: e.tensor_tensor(out=h3(KS), in0=k3, in1=v3(EK), op=ALU.mult),
                     reads=B["U1"][4:8] + [B["FP"][EK]], writes=[B["HP"][KS]])
                S.op("dve", lambda e, q3=q3: e.tensor_tensor(out=h3(QB), in0=q3, in1=v3(EB), op=ALU.mult),
                     reads=B["U1"][0:4] + [B["FP"][EB]], writes=[B["HP"][QB]])
                fill("B", s)
                for h in range(4):
                    S.op("pe", lambda e, h=h, cs=cs: e.transpose(out=pb[2][:, h * 128:(h + 1) * 128],
                                                                 in_=U1[:, 4 + h, cs], identity=identf),
                         reads=[B["U1"][4 + h], B["cst"]], writes=[PB[2]])
                S.op("dve", lambda e: e.tensor_tensor(out=HP[:, KD, :], in0=pb[2][:], in1=FP[:, EKD, :], op=ALU.mult),
                     reads=[PB[2], B["FP"][EKD]], writes=[B["HP"][KD]])
                for h in range(4):
                    S.op("pe", lambda e, h=h: e.matmul(pb[3][:, h * 128:(h + 1) * 128],
                                                       lhsT=HP[:, KS, h * 128:(h + 1) * 128],
                                                       rhs=HP[:, QS, h * 128:(h + 1) * 128], start=True, stop=True),
                         reads=[B["HP"][KS], B["HP"][QS]], writes=[PB[3]])
                S.op("dve", lambda e: e.tensor_tensor(out=HP[:, ATM, :], in0=pb[3][:], in1=maskT, op=ALU.mult),
                     reads=[PB[3], B["cst"]], writes=[B["HP"][ATM]])
                fill("C", s)
                for h in range(4):
                    for c2 in range(2):
                        cc = h * 2 + c2
                        ob = 4 + cc // 4
                        oc = (cc % 4) * 128
                        S.op("pe", lambda e, h=h, c2=c2, ob=ob, oc=oc, s=s: e.matmul(
                            pb[ob][:, oc:oc + 128],
                            lhsT=vT[:, s, h * 256 + c2 * 128:h * 256 + (c2 + 1) * 128],
                            rhs=HP[:, ATM, h * 128:(h + 1) * 128], start=True, stop=False),
                            reads=[B["vT"][s], B["HP"][ATM]], writes=[PB[ob]])
                        S.op("pe", lambda e, h=h, c2=c2, ob=ob, oc=oc, sbr=sbr: e.matmul(
                            pb[ob][:, oc:oc + 128],
                            lhsT=Sbf[:, sbr, h, c2 * 128:(c2 + 1) * 128],
                            rhs=HP[:, QB, h * 128:(h + 1) * 128], start=False, stop=True),
                            reads=[B["Sbf"][sbr], B["HP"][QB]], writes=[PB[ob]])
                for h in range(4):
                    kb = 2 + h // 2
                    kc0 = (h % 2) * 256
                    S.op("pe", lambda e, h=h, kb=kb, kc0=kc0, s=s: e.matmul(
                        pb[kb][:, kc0:kc0 + 256], lhsT=HP[:, KD, h * 128:(h + 1) * 128],
                        rhs=vT[:, s, h * 256:(h + 1) * 256], start=True, stop=True),
                        reads=[B["HP"][KD], B["vT"][s]], writes=[PB[kb]])
                for h in range(4):
                    kb = 2 + h // 2
                    kc0 = (h % 2) * 256
                    S.op("dve", lambda e, h=h, kb=kb, kc0=kc0: e.scalar_tensor_tensor(
                        out=Sst[:, h, :], in0=Sst[:, h, :], scalar=FP[:, EB, h * 128 + 127:h * 128 + 128],
                        in1=pb[kb][:, kc0:kc0 + 256], op0=ALU.mult, op1=ALU.add),
                        reads=[B["Sst"], B["FP"][EB], PB[kb]], writes=[B["Sst"]])
                S.op("pool", lambda e, sbw=sbw: e.tensor_copy(out=Sbf[:, sbw, :, :], in_=Sst[:]),
                     reads=[B["Sst"]], writes=[B["Sbf"][sbw]])
                for ob in range(2):
                    S.op("act", lambda e, ob=ob: e.activation(out=FP[:, OS0 + ob, :], in_=pb[4 + ob][:], func=AF.Copy),
                         reads=[PB[4 + ob]], writes=[B["FP"][OS0 + ob]])
                    S.op("act", lambda e, ob=ob: e.activation(out=HP[:, OQ0 + ob, :], in_=pb[4 + ob][:], func=AF.Square),
                         reads=[PB[4 + ob]], writes=[B["HP"][OQ0 + ob]])
                fill("D", s)
                for h in range(4):
                    for c2 in range(2):
                        cc = h * 2 + c2
                        S.op("pe", lambda e, h=h, c2=c2, cc=cc: e.matmul(
                            pb[4][:, h * 128:(h + 1) * 128], lhsT=onesb[:],
                            rhs=HP[:, OQ0 + cc // 4, (cc % 4) * 128:(cc % 4 + 1) * 128],
                            start=(c2 == 0), stop=(c2 == 1)),
                            reads=[B["onesb"], B["HP"][OQ0 + cc // 4]], writes=[PB[4]])
                S.op("act", lambda e: e.activation(out=FP[:, RO, :], in_=pb[4][:], func=AF.Ln, scale=1.0 / 256,
                                                   bias=pp[:, 27:28]),
                     reads=[PB[4], B["pp"]], writes=[B["FP"][RO]])
                S.op("act", lambda e: e.activation(out=FP[:, RO, :], in_=FP[:, RO, :], func=AF.Exp, scale=-0.5),
                     reads=[B["FP"][RO]], writes=[B["FP"][RO]])
                osv = FP[:, OS0:OS0 + 2, :].rearrange("p a (b c t) -> p (a b) c t", b=2, c=2)
                tnv4 = FP[:, TN0:TN0 + 2, :].rearrange("p a (b c t) -> p (a b) c t", b=2, c=2)
                tnv = FP[:, TN0:TN0 + 2, :].rearrange("p a (b t) -> p (a b) t", b=4)
                rov = FP[:, RO, :].rearrange("p (h t) -> p h t", h=4)
                for c2 in range(2):
                    S.op("dve", lambda e, c2=c2, osv=osv, tnv4=tnv4, rov=rov: e.scalar_tensor_tensor(
                        out=tnv4[:, :, c2, :], in0=osv[:, :, c2, :], scalar=pp[:, 24 + c2:25 + c2], in1=rov,
                        op0=ALU.mult, op1=ALU.mult),
                        reads=[B["FP"][OS0], B["FP"][OS1], B["FP"][RO], B["pp"]],
                        writes=[B["FP"][TN0], B["FP"][TN1]])
                S.op("dve", lambda e, cs=cs, tnv=tnv: e.tensor_tensor(out=agla[:, :, cs], in0=tnv, in1=sgr[:, :, cs],
                                                                      op=ALU.mult),
                     reads=[B["FP"][TN0], B["FP"][TN1]] + B["sgr"], writes=B["agla"])

            while mc_next[0] < 8:
                mc_raw()

            set_pbanks([0, 1, 2, 3])
            for c in range(8):
                g1, t1, g2 = c % 2, 2 + c % 2, 4 + c % 2
                S.op("act", lambda e, c=c, g1=g1: e.activation(out=FP[:, g1, :], in_=yb[:, c, :], func=AF.Sigmoid),
                     reads=[B["yb"][c]], writes=[B["FP"][g1]])
                by = proj_from(t, U_WCO(c), aconv, B["aconv"])
                S.op("dve", lambda e, by=by, g1=g1, t1=t1: e.tensor_tensor(out=FP[:, t1, :], in0=pb[by][:],
                                                                           in1=FP[:, g1, :], op=ALU.mult),
                     reads=[PB[by], B["FP"][g1]], writes=[B["FP"][t1]])
                bm2 = proj_fm(t, U_MG(c), hbuf)
                S.op("act", lambda e, bm2=bm2, g2=g2: e.activation(out=FP[:, g2, :], in_=pb[bm2][:], func=AF.Sigmoid),
                     reads=[PB[bm2]], writes=[B["FP"][g2]])
                by2 = proj_from(t, U_WGO(c), agla, B["agla"])
                S.op("dve", lambda e, by2=by2, g2=g2: e.tensor_tensor(out=FP[:, g2, :], in0=pb[by2][:],
                                                                      in1=FP[:, g2, :], op=ALU.mult),
                     reads=[PB[by2], B["FP"][g2]], writes=[B["FP"][g2]])
                S.op("pool", lambda e, c=c, t1=t1, g2=g2: e.tensor_tensor(out=yb[:, c, :], in0=FP[:, t1, :],
                                                                          in1=FP[:, g2, :], op=ALU.add),
                     reads=[B["FP"][t1], B["FP"][g2]], writes=[B["yb"][c]])

            wo = [wide(t, 2), None]
            wo[1] = wide(t, 3)
            for s in range(4):
                f = cnt["fin"]
                cnt["fin"] += 1
                p = f % 2
                xr0 = 6 + p * 2
                ob0 = p * 2
                r0 = t * T + s * 128
                xrv = FP[:, xr0:xr0 + 2, :].rearrange("p a t -> p (a t)")
                obv = FP[:, ob0:ob0 + 2, :].rearrange("p a t -> p (a t)")
                S.op("act", lambda e, xrv=xrv, r0=r0: e.dma_start(out=xrv, in_=x_d[r0:r0 + 128, :]),
                     writes=[B["FP"][xr0], B["FP"][xr0 + 1]], dma_sem=xr_sems[p])
                for half in range(2):
                    wap, wbuf = wo[half]
                    bi = next_pbank()
                    for kc in range(8):
                        S.op("pe", lambda e, bi=bi, kc=kc, s=s, wap=wap: e.matmul(
                            pb[bi][:], lhsT=yb[:, kc, s * 128:(s + 1) * 128],
                            rhs=wap[:, kc * 512:(kc + 1) * 512], start=(kc == 0), stop=(kc == 7)),
                            reads=[wbuf] + B["yb"], writes=[PB[bi]])
                    S.op("dve", lambda e, bi=bi, half=half, xr0=xr0: e.tensor_tensor(
                        out=FP[:, xr0 + half, :], in0=pb[bi][:], in1=FP[:, xr0 + half, :], op=ALU.add),
                        reads=[PB[bi], B["FP"][xr0 + half]], writes=[B["FP"][xr0 + half]])
                col = 8 + (f % 8)
                jv = HP[:, 0:2, :].rearrange("p a t -> p (a t)")
                S.op("act", lambda e, xrv=xrv, jv=jv, col=col: e.activation(
                    out=jv, in_=xrv, func=AF.Square, accum_out=ssx[:, col:col + 1]),
                    reads=[B["FP"][xr0], B["FP"][xr0 + 1]], writes=[B["HP"][0], B["HP"][1], B["ssx"][col]])
                S.op("pool", lambda e, col=col: e.tensor_scalar(
                    out=ssx[:, col:col + 1], in0=ssx[:, col:col + 1], scalar1=1.0 / D, scalar2=EPS,
                    op0=ALU.mult, op1=ALU.add), reads=[B["ssx"][col]], writes=[B["ssx"][col]])
                S.op("pool", lambda e, col=col: e.tensor_tensor(
                    out=ssx[:, col:col + 1], in0=ssx[:, col:col + 1], in1=pp[:, 26:27], op=ALU.pow),
                    reads=[B["ssx"][col], B["pp"]], writes=[B["ssx"][col]])
                S.op("dve", lambda e, xrv=xrv, obv=obv, col=col: e.scalar_tensor_tensor(
                    out=obv, in0=xrv, scalar=ssx[:, col:col + 1], in1=Gf[:], op0=ALU.mult, op1=ALU.mult),
                    reads=[B["FP"][xr0], B["FP"][xr0 + 1], B["ssx"][col], B["Gf"]],
                    writes=[B["FP"][ob0], B["FP"][ob0 + 1]])
                S.op("sp", lambda e, obv=obv, r0=r0: e.dma_start(out=out_d[r0:r0 + 128, :], in_=obv),
                     reads=[B["FP"][ob0], B["FP"][ob0 + 1]], writes=[B["outdone"][p]], dma_sem=out_sems[p])

        issue_casts(n_units_nar=7)
        for s_ in range(4):
            pre_elem(0, s_)
        issue_casts(n_units_nar=21)
        for s_ in range(4):
            pre_pe(0, s_)
        for t in range(n_tiles):
            tile(t)
        S.op("sp", None, reads=B["outdone"], writes=B["outdone"])
        S.finalize_counts()

        dma_sems = {n: es.enter_context(nc.semaphore(n)) for n in dma_sem_names}
        with nc.Block() as block:
            @block.tensor
            def _(e):
                S.replay("pe", e, eng_sems, dma_sems)

            @block.scalar
            def _(e):
                S.replay("act", e, eng_sems, dma_sems)

            @block.vector
            def _(e):
                S.replay("dve", e, eng_sems, dma_sems)

            @block.gpsimd
            def _(e):
                S.replay("pool", e, eng_sems, dma_sems)

            @block.sync
            def _(e):
                S.replay("sp", e, eng_sems, dma_sems)
    return nc, S


def _unit(W, col0, ncols, width):
    blk = np.zeros((1024, width), np.float32)
    blk[:, :ncols] = W[:, col0:col0 + ncols]
    return blk.reshape(8, 128, width).transpose(1, 0, 2).reshape(128, 8 * width)


def _pack_weights(w_in, w_conv_out, w_gla_out, w_out):
    nar = np.zeros((N_NAR, 128, 1024), np.float32)
    wide = np.zeros((N_WIDE, 128, 4096), np.float32)
    o_cval, o_cgate, o_cz, o_q, o_k, o_v, o_glr, o_gr, o_mc, o_mg = 0, 1024, 2048, 3072, 3584, 4096, 5120, 5136, 6160, 7184
    nar[U_GLR] = _unit(w_in, o_glr, 16, 128)
    for c in range(8):
        nar[U_CG(c)] = _unit(w_in, o_cgate + c * 128, 128, 128)
        nar[U_CV(c)] = _unit(w_in, o_cval + c * 128, 128, 128)
        nar[U_CZ(c)] = _unit(w_in, o_cz + c * 128, 128, 128)
        nar[U_GR(c)] = _unit(w_in, o_gr + c * 128, 128, 128)
        nar[U_MC(c)] = _unit(w_in, o_mc + c * 128, 128, 128)
        nar[U_MG(c)] = _unit(w_in, o_mg + c * 128, 128, 128)
        nar[U_WCO(c)] = _unit(w_conv_out, c * 128, 128, 128)
        nar[U_WGO(c)] = _unit(w_gla_out, c * 128, 128, 128)
    for h in range(4):
        nar[U_Q(h)] = _unit(w_in, o_q + h * 128, 128, 128)
        nar[U_K(h)] = _unit(w_in, o_k + h * 128, 128, 128)
    wide[0] = _unit(w_in, o_v, 512, 512)
    wide[1] = _unit(w_in, o_v + 512, 512, 512)
    wide[2] = _unit(w_out, 0, 512, 512)
    wide[3] = _unit(w_out, 512, 512, 512)
    return nar, wide


def _const_tables():
    j = np.arange(128)[:, None]
    i = np.arange(128)[None, :]
    ident = (j == i).astype(np.float32)
    le = (j <= i).astype(np.float32)
    Lf = -le / 16.0
    Lr = -(le - (j <= 63).astype(np.float32)) / 16.0
    Ur = -(j > i).astype(np.float32) / 16.0
    maskT = np.tile(le, (1, 4))
    p = np.arange(128)[:, None]
    istack = np.tile(((p % 32) == np.arange(32)[None, :]).astype(np.float32), (1, 8))
    return np.concatenate([ident, Lf, Lr, Ur, maskT, istack], axis=1).astype(np.float32)


def _prep_inputs(x, norm_g, w_in, conv_w, conv_b, conv_ln_g, conv_ln_b, w_conv_out,
                 gate_w2, gate_b, gla_norm_g, w_gla_out, w_out, final_g):
    f = lambda a: np.ascontiguousarray(np.asarray(a, dtype=np.float32))
    x = f(x)
    nar, wide = _pack_weights(f(w_in)[0], f(w_conv_out)[0], f(w_gla_out)[0], f(w_out)[0])
    pp = np.zeros((128, 32), np.float32)
    pp[:, 0:8] = f(conv_b)[0].reshape(8, 128).T
    pp[:, 8:16] = f(conv_ln_g)[0].reshape(8, 128).T
    pp[:, 16:24] = f(conv_ln_b)[0].reshape(8, 128).T
    pp[:, 24:26] = f(gla_norm_g)[0].reshape(2, 128).T
    pp[:, 26] = -0.5
    pp[:, 27] = EPS
    cwp = np.zeros((32, D), np.float32)
    cwp[:KTAPS] = f(conv_w)[0]
    cw = np.ascontiguousarray(cwp.reshape(8, 4, 8, 4, 32).transpose(1, 4, 2, 3, 0)).reshape(128, 8, 4, 8)
    gw2b = np.zeros((32, 512), np.float32)
    gw2b[0:16] = f(gate_w2)[0]
    gw2b[16] = f(gate_b)[0]
    shared = {
        "wnar": nar, "wwide": wide, "pp": pp, "cw": cw.reshape(128, 256), "gw2b": gw2b,
        "ng": f(norm_g)[0], "fg": f(final_g), "cst": _const_tables(),
    }
    xs = x.reshape(NCORE, TOK_CORE, D)
    return [dict(shared, x=np.ascontiguousarray(xs[c])) for c in range(NCORE)]


def kernel(x, norm_g, w_in, conv_w, conv_b, conv_ln_g, conv_ln_b, w_conv_out,
           gate_w2, gate_b, gla_norm_g, w_gla_out, w_out, final_g):
    in_maps = _prep_inputs(x, norm_g, w_in, conv_w, conv_b, conv_ln_g, conv_ln_b, w_conv_out,
                           gate_w2, gate_b, gla_norm_g, w_gla_out, w_out, final_g)
    nc, _ = build_nc()
    res = run_bass_kernel_spmd(nc, in_maps, core_ids=list(range(NCORE)))
    out = np.stack([np.asarray(r["out"], dtype=np.float32) for r in res.results], axis=0)
    return out.reshape(16, SEQ, D)
```
